# Optimizing a Trainium2 kernel written in Bass

```python
import jax, jax.numpy as jnp
from jax import lax
import numpy as np

D_MODEL = 1024
BATCH = 16
SEQ = 2048
DEPTH = 1

GRID_W = 64
Q_BLOCK = 128
ROPE_THETA = 10000.0
EPS = 1e-6
H_A = 8
QK_NOPE = 64
QK_ROPE = 32
V_DIM_A = 64
Q_LORA = 256
KV_LORA = 128
H_B = 8
KV_B = 2
HD_B = 64
D_FF = 4 * D_MODEL
PLE_DIM = 256
IN_SPLITS = [Q_LORA, KV_LORA + QK_ROPE, H_B * HD_B, KV_B * HD_B, KV_B * HD_B, D_MODEL, D_MODEL]
D_IN = sum(IN_SPLITS)

kernel_name = "hybrid_mla_axial_gqa_gated_block"


def rmsnorm(x, g):
    xf = x.astype(jnp.float32)
    y = xf * lax.rsqrt(jnp.mean(xf * xf, axis=-1, keepdims=True) + EPS)
    return (y * g.astype(jnp.float32)).astype(x.dtype)


def rope_angles(pos, dim):
    inv = ROPE_THETA ** (-jnp.arange(0, dim, 2, dtype=jnp.float32) / dim)
    return pos.astype(jnp.float32)[:, None] * inv[None, :]


def apply_rope(x, ang):
    cos = jnp.cos(ang)[None, :, None, :].astype(x.dtype)
    sin = jnp.sin(ang)[None, :, None, :].astype(x.dtype)
    x1, x2 = jnp.split(x, 2, axis=-1)
    return jnp.concatenate([x1 * cos - x2 * sin, x1 * sin + x2 * cos], axis=-1)


def blocked_attention(q, k, v):
    B, S, Hk, G, dk = q.shape
    nb = S // Q_BLOCK
    qb = q.reshape(B, nb, Q_BLOCK, Hk, G, dk).transpose(1, 0, 2, 3, 4, 5)

    def one_block(qblk):
        s = jnp.einsum("bqhgd,bkhd->bhgqk", qblk, k).astype(jnp.float32)
        pr = jax.nn.softmax(s, axis=-1).astype(v.dtype)
        return jnp.einsum("bhgqk,bkhd->bqhgd", pr, v)

    o = lax.map(one_block, qb)
    return o.transpose(1, 0, 2, 3, 4, 5).reshape(B, S, Hk * G, v.shape[-1])


def setup_inputs(seed: int = 0) -> dict:
    key = jax.random.key(seed)
    ks = jax.random.split(key, 24)
    f32 = jnp.float32

    def w(k, shape, fan_in):
        return jax.random.normal(k, shape, f32) * (fan_in ** -0.5)

    def gain(k, shape):
        return 1.0 + 0.05 * jax.random.normal(k, shape, f32)

    L = DEPTH
    return {
        "x": jax.random.normal(ks[0], (BATCH, SEQ, D_MODEL), f32),
        "p": jax.random.normal(ks[1], (DEPTH, BATCH, SEQ, PLE_DIM), f32),
        "g_mix": gain(ks[2], (L, D_MODEL)),
        "w_in": w(ks[3], (L, D_MODEL, D_IN), D_MODEL),
        "g_qa": gain(ks[4], (L, Q_LORA)),
        "w_qb": w(ks[5], (L, Q_LORA, H_A * (QK_NOPE + QK_ROPE)), Q_LORA),
        "g_kva": gain(ks[6], (L, KV_LORA)),
        "w_kvb": w(ks[7], (L, KV_LORA, H_A * (QK_NOPE + V_DIM_A)), KV_LORA),
        "g_qn": gain(ks[8], (L, HD_B)),
        "g_kn": gain(ks[9], (L, HD_B)),
        "w_oa": w(ks[10], (L, H_A * V_DIM_A, D_MODEL), H_A * V_DIM_A),
        "w_ob": w(ks[11], (L, H_B * HD_B, D_MODEL), H_B * HD_B),
        "w_o": w(ks[12], (L, D_MODEL, D_MODEL), D_MODEL),
        "g_mlp": gain(ks[13], (L, D_MODEL)),
        "w_up": w(ks[14], (L, D_MODEL, D_FF), D_MODEL),
        "w_down": w(ks[15], (L, D_FF, D_MODEL), D_FF),
        "g_ple": gain(ks[16], (L, D_MODEL)),
        "w_ple_gate": w(ks[17], (L, D_MODEL, D_MODEL), D_MODEL),
        "w_ple": w(ks[18], (L, PLE_DIM, D_MODEL), PLE_DIM),
        "g_final": gain(ks[19], (D_MODEL,)),
    }


def reference(x, p, g_mix, w_in, g_qa, w_qb, g_kva, w_kvb, g_qn, g_kn, w_oa, w_ob, w_o,
              g_mlp, w_up, w_down, g_ple, w_ple_gate, w_ple, g_final):
    B, S, D = x.shape
    ROWS = S // GRID_W
    t = jnp.arange(S)
    row = jnp.broadcast_to(jnp.arange(ROWS)[:, None], (ROWS, GRID_W)).reshape(-1)
    col = jnp.broadcast_to(jnp.arange(GRID_W)[None, :], (ROWS, GRID_W)).reshape(-1)
    ang_1d = rope_angles(t, QK_ROPE)
    ang_row = rope_angles(row, HD_B // 2)
    ang_col = rope_angles(col, HD_B // 2)
    split_idx = list(np.cumsum(IN_SPLITS)[:-1])
    scale_a = (QK_NOPE + QK_ROPE) ** -0.5
    scale_b = HD_B ** -0.5

    for i in range(DEPTH):
        h = rmsnorm(x, g_mix[i])
        z = jnp.einsum("bsd,de->bse", h, w_in[i])
        q_lat, kv_lat, qb, kb, vb, gate_a, gate_b = jnp.split(z, split_idx, axis=-1)

        cq = rmsnorm(q_lat, g_qa[i])
        qa = jnp.einsum("bsr,re->bse", cq, w_qb[i]).reshape(B, S, H_A, QK_NOPE + QK_ROPE)
        qa_nope, qa_rope = jnp.split(qa, [QK_NOPE], axis=-1)
        qa_rope = apply_rope(qa_rope, ang_1d)
        c_kv, k_pe = jnp.split(kv_lat, [KV_LORA], axis=-1)
        c_kv = rmsnorm(c_kv, g_kva[i])
        k_pe = apply_rope(k_pe[:, :, None, :], ang_1d)
        kva = jnp.einsum("bsr,re->bse", c_kv, w_kvb[i]).reshape(B, S, H_A, QK_NOPE + V_DIM_A)
        ka_nope, va = jnp.split(kva, [QK_NOPE], axis=-1)
        qa_full = (jnp.concatenate([qa_nope, qa_rope], axis=-1) * scale_a)[:, :, :, None, :]
        ka_full = jnp.concatenate([ka_nope, jnp.broadcast_to(k_pe, (B, S, H_A, QK_ROPE))], axis=-1)
        oa = blocked_attention(qa_full, ka_full, va).reshape(B, S, H_A * V_DIM_A)
        ya = jnp.einsum("bse,ed->bsd", oa, w_oa[i])

        qb = rmsnorm(qb.reshape(B, S, H_B, HD_B), g_qn[i])
        kb = rmsnorm(kb.reshape(B, S, KV_B, HD_B), g_kn[i])
        vb = vb.reshape(B, S, KV_B, HD_B)
        qr, qc = jnp.split(qb, 2, axis=-1)
        qb = jnp.concatenate([apply_rope(qr, ang_row), apply_rope(qc, ang_col)], axis=-1)
        kr, kc = jnp.split(kb, 2, axis=-1)
        kb = jnp.concatenate([apply_rope(kr, ang_row), apply_rope(kc, ang_col)], axis=-1)
        qb = (qb * scale_b).reshape(B, S, KV_B, H_B // KV_B, HD_B)
        ob = blocked_attention(qb, kb, vb).reshape(B, S, H_B * HD_B)
        yb = jnp.einsum("bse,ed->bsd", ob, w_ob[i])

        merged = jax.nn.sigmoid(gate_a) * ya + jax.nn.sigmoid(gate_b) * yb
        x = x + jnp.einsum("bsd,de->bse", merged, w_o[i])

        h2 = rmsnorm(x, g_mlp[i])
        u = jax.nn.relu(jnp.einsum("bsd,df->bsf", h2, w_up[i]))
        x = x + jnp.einsum("bsf,fd->bsd", u * u, w_down[i])

        h3 = rmsnorm(x, g_ple[i])
        gate = jax.nn.sigmoid(jnp.einsum("bsd,de->bse", h3, w_ple_gate[i]))
        x = x + gate * jnp.einsum("bsk,kd->bsd", p[i], w_ple[i])

    return rmsnorm(x, g_final)
```

```python
import os
import numpy as np
import ml_dtypes
from contextlib import ExitStack
import concourse.bass as bass
import concourse.mybir as mybir
from concourse.bass_utils import run_bass_kernel_spmd

F32 = mybir.dt.float32
BF16 = mybir.dt.bfloat16
AF = mybir.ActivationFunctionType
ALU = mybir.AluOpType

N_CORES = 8
LOOKA = 2
NDUMMY_A = 0
BOLD = False
S = 2048
D = 1024
NT = S // 128
NSEQ = 2
EPS = 1e-6
SCALE_A = 96.0 ** -0.5
SCALE_B = 64.0 ** -0.5
ENGS = ("pe", "act", "dve", "pool", "sp")

C_QLAT, C_CKV, C_QB, C_KB, C_VB, C_QBSW, C_KBSW, C_END = 0, 256, 384, 896, 1024, 1152, 1664, 1792
G_MIX, G_MLP, G_PLE, G_QA, G_KVA, G_QN, G_QNSW, G_KN, G_KNSW = 0, 8, 16, 24, 26, 27, 28, 29, 30


class Buf:
    __slots__ = ("name", "last_w", "readers", "dsem", "dcnt")

    def __init__(self, name):
        self.name = name
        self.last_w = None
        self.readers = []
        self.dsem = None
        self.dcnt = 0


class Sched:
    def __init__(self, nc, stack):
        self.nc = nc
        self.stack = stack
        self.eng = {"pe": nc.tensor, "act": nc.scalar, "dve": nc.vector, "pool": nc.gpsimd, "sp": nc.sync}
        self.sem = {e: stack.enter_context(nc.semaphore("prog_" + e)) for e in ENGS}
        self.cnt = {e: 0 for e in ENGS}
        self.seen = {e: {} for e in ENGS}
        self.nsem = 0
        self.dbufs = []

    def buf(self, name):
        return Buf(name)

    def bufs(self, name, n):
        return [Buf("%s%d" % (name, i)) for i in range(n)]

    def _dep_key(self, dep):
        if dep[0] == "e":
            return ("e", dep[1]), self.sem[dep[1]], dep[2]
        return ("d", id(dep[1])), dep[1], dep[2]

    def _wait_all(self, eng, deps):
        need = {}
        for dep in deps:
            if dep is None:
                continue
            if dep[0] == "e" and dep[1] == eng and eng == "pe":
                continue
            key, sem, val = self._dep_key(dep)
            if self.seen[eng].get(key, 0) >= val:
                continue
            if key not in need or need[key][1] < val:
                need[key] = (sem, val)
        for key, (sem, val) in need.items():
            self.eng[eng].wait_ge(sem, val)
            self.seen[eng][key] = val

    @staticmethod
    def _deps(reads, writes):
        deps = []
        for b in reads:
            deps.append(b.last_w)
        for b in writes:
            deps.append(b.last_w)
            deps.extend(b.readers)
        return deps

    def op(self, eng, fn, reads=(), writes=()):
        self._wait_all(eng, self._deps(reads, writes))
        inst = fn(self.eng[eng])
        self.cnt[eng] += 1
        inst.then_inc(self.sem[eng], 1)
        tag = ("e", eng, self.cnt[eng])
        for b in reads:
            b.readers = [r for r in b.readers if not (r[0] == "e" and r[1] == eng)]
            b.readers.append(tag)
        for b in writes:
            b.last_w = tag
            b.readers = []
        return inst

    def dma(self, out_ap, in_ap, reads=(), writes=(), q="sp", sem_buf=None):
        self._wait_all(q, self._deps(reads, writes))
        sb = sem_buf if sem_buf is not None else (writes[0] if writes else reads[0])
        if sb.dsem is None:
            sb.dsem = self.stack.enter_context(self.nc.semaphore("dma%d" % self.nsem))
            self.nsem += 1
            self.dbufs.append(sb)
        inst = self.eng[q].dma_start(out=out_ap, in_=in_ap)
        sb.dcnt += 16
        inst.then_inc(sb.dsem, 16)
        tag = ("d", sb.dsem, sb.dcnt)
        for b in reads:
            b.readers.append(tag)
        for b in writes:
            b.last_w = tag
            b.readers = []
        return inst

    def barrier(self):
        deps = [("e", e, self.cnt[e]) for e in ENGS if self.cnt[e] > 0]
        deps += [("d", b.dsem, b.dcnt) for b in self.dbufs if b.dcnt > 0 and not b.name.startswith("scr_")]
        for e in ENGS:
            self._wait_all(e, deps)


def build_program(stop=None):
    nc = bass.Bass("TRN2", target_bir_lowering=False)

    def din(name, shape, dt=F32):
        return nc.dram_tensor(name, shape, dt, kind="ExternalInput").ap()

    x_d = din("x", [NSEQ * S, D])
    p_d = din("p", [NSEQ * S, 256])
    WSHAPES = dict(wprep=[D, C_END], wkpe=[D, 256], wgate=[D, 2048], wqb=[256, 768], wqbsw=[256, 768],
                   wkvb=[128, 1024], woa=[512, D], wob=[512, D], wo=[D, D], wup=[D, 4096], wdown=[4096, D],
                   wpg=[D, D], wple=[256, D])
    WORDER = ["wprep", "wkpe", "wqb", "wqbsw", "wkvb", "wgate", "woa", "wob", "wo", "wpg", "wple", "wup", "wdown"]
    w_f32 = {k: din(k, WSHAPES[k]) for k in WORDER}
    w_bf = {k: nc.dram_tensor(k + "_bf", WSHAPES[k], BF16, kind="Internal").ap() for k in ("wgate", "woa", "wob", "wo", "wpg", "wple", "wup", "wdown")}
    gvec_d = din("gvec", [128, 32])
    gfin_d = din("gfin", [128, D])
    tabs_d = din("tabs", [4, 128, S], BF16)
    cst_d = din("cst", [4, 128, 128], BF16)
    out_d = nc.dram_tensor("out", [NSEQ * S, D], F32, kind="ExternalOutput").ap()

    with ExitStack() as st:
        Sd = Sched(nc, st)

        def sb(name, shape, dt):
            return st.enter_context(nc.sbuf_tensor("sb_" + name, shape, dt))

        cst = sb("cst", [128, 4, 128], BF16)
        ident, ones_blk, ones_all, zeros_w = cst[:, 0, :], cst[:, 1, :], cst[:, 2, :], cst[:, 3, :]
        gvec = sb("gvec", [128, 32], F32)
        gfin = sb("gfin", [128, D], F32)
        stat = sb("stat", [128, 64], F32)
        KB = 1024
        ARENA_BYTES = 200 * KB
        HT_OFF, TABS_OFF, GEN_OFF, MERGED_OFF = 0, 32 * KB, 48 * KB, 168 * KB
        arena = sb("arena", [128, ARENA_BYTES // 2], BF16)
        PS = st.enter_context(nc.psum_tensor("PS", [128, 8, 512], F32))

        B_const = Sd.buf("const")
        B_stat = Sd.buf("stat")
        B_ps = Sd.bufs("ps", 8)
        B_scr = {k: Sd.buf("scr_" + k) for k in WORDER}

        class Arena:
            def __init__(self):
                self.off = 0
                self.lim = ARENA_BYTES

            def at(self, off, lim=ARENA_BYTES):
                self.off = off
                self.lim = lim

            def take(self, shape, dt):
                n = 1
                for s_ in shape[1:]:
                    n *= s_
                nb = n * (4 if dt == F32 else 2)
                nb = (nb + 63) // 64 * 64
                assert self.off + nb <= self.lim, (self.off, nb, self.lim)
                v = arena[:, self.off // 2:(self.off + nb) // 2]
                self.off += nb
                if dt == F32:
                    v = v.bitcast(F32)
                v = v[:, 0:n]
                if len(shape) == 3:
                    v = v.rearrange("p (a b) -> p a b", a=shape[1])
                elif len(shape) == 4:
                    v = v.rearrange("p (a b c) -> p a b c", a=shape[1], b=shape[2])
                return v

        AR = Arena()
        PEL = "dve" if os.environ.get("MK_NOPOOL") else "pool"

        def ps(b):
            return PS[:, b, :]

        def ps2(b):
            return PS[:, b:b + 2, :].rearrange("p a b -> p (a b)")

        def ps_bf(b):
            return PS[:, b, :].bitcast(BF16)

        def mm(e, out, terms):
            n = len(terms)
            ins = None
            for i, (l, r) in enumerate(terms):
                ins = e.matmul(out, l, r, start=(i == 0), stop=(i == n - 1))
            return ins

        def kview(ap):
            return ap.rearrange("(c p) n -> p c n", p=128)

        Sd.dma(cst[:], cst_d.rearrange("a p n -> p a n"), writes=[B_const])
        Sd.dma(gvec[:], gvec_d, writes=[B_const])
        Sd.dma(gfin[:], gfin_d, writes=[B_const])
        def convert(k, after=()):
            rows = WSHAPES[k][0]
            nsplit = 4 if rows * WSHAPES[k][1] >= 4 * 1024 * 1024 else 1
            step = rows // nsplit
            for i in range(nsplit):
                Sd.dma(w_bf[k][i * step:(i + 1) * step, :], w_f32[k][i * step:(i + 1) * step, :],
                       reads=list(after) if i == 0 else [], writes=[B_scr[k]] if i == 0 else [], sem_buf=B_scr[k], q="pool")
            B_scr[k].last_w = ("d", B_scr[k].dsem, B_scr[k].dcnt)

        SCRATCH = ("wgate", "woa", "wob", "wo", "wpg", "wple", "wup", "wdown")

        def wload(dst, key, src_view, bufw, q="sp"):
            Sd.dma(dst, src_view, reads=[B_scr[key]], writes=[bufw], q=q, sem_buf=bufw)

        def wload_cast(dst, src_view, bufw):
            Sd.dma(dst, src_view, writes=[bufw], q="pool", sem_buf=bufw)

        def rstd_batch(ss_ap, n_feat, reads, writes):
            Sd.op("dve", lambda e: e.tensor_scalar(out=ss_ap, in0=ss_ap, scalar1=1.0 / n_feat, scalar2=EPS,
                                                   op0=ALU.mult, op1=ALU.add), reads=reads, writes=writes)
            Sd.op("act", lambda e: e.activation(out=ss_ap, in_=ss_ap, func=AF.Sqrt), reads=writes, writes=writes)
            Sd.op("dve", lambda e: e.reciprocal(out=ss_ap, in_=ss_ap), reads=writes, writes=writes)

        def norm_transpose(xsrc, B_x, ntile, rstd_ap, B_rstd, gcol, dstT, B_dst, xn, B_xn, psbanks):
            if rstd_ap is not None:
                for t in range(ntile):
                    if t % 2 == 0:
                        Sd.op("act", lambda e, t=t: e.activation(out=xn[:, t, :], in_=xsrc[:, t, :], func=AF.Identity,
                                                                 scale=rstd_ap[:, t:t + 1]),
                              reads=[B_x[t], B_rstd], writes=[B_xn[t]])
                    else:
                        Sd.op("dve", lambda e, t=t: e.tensor_scalar(out=xn[:, t, :], in0=xsrc[:, t, :], scalar1=rstd_ap[:, t:t + 1],
                                                                    scalar2=None, op0=ALU.mult),
                              reads=[B_x[t], B_rstd], writes=[B_xn[t]])
            ngrp = (ntile + 7) // 8
            k = 0
            for g in range(ngrp):
                t0 = g * 8
                tn = min(8, ntile - t0)
                for c in range(8):
                    bk = psbanks[k % len(psbanks)]
                    k += 1

                    def tr(e, c=c, t0=t0, tn=tn, bk=bk):
                        ins = None
                        for i in range(tn):
                            ins = e.transpose(out=ps_bf(bk)[:, i * 128:(i + 1) * 128],
                                              in_=xn[:, t0 + i, c * 128:(c + 1) * 128], identity=ident)
                        return ins
                    Sd.op("pe", tr, reads=[B_xn[t0 + i] for i in range(tn)] + [B_const], writes=[B_ps[bk]])
                    dst_bufs = sorted(set(B_dst[(t0 + i) // 4] for i in range(tn)), key=id)
                    Sd.op("dve", lambda e, c=c, t0=t0, tn=tn, bk=bk: e.tensor_scalar(
                        out=dstT[:, c, t0 * 128:(t0 + tn) * 128], in0=ps_bf(bk)[:, 0:tn * 128],
                        scalar1=gvec[:, gcol + c:gcol + c + 1], scalar2=None, op0=ALU.mult),
                        reads=[B_ps[bk], B_const], writes=dst_bufs)

        XALL_OFF = GEN_OFF + 44 * KB
        pre_x = None
        for b in range(NSEQ):
            if stop is not None and (b > 0 or stop == 0):
                break
            row0 = b * S
            Sd.barrier()
            AR.at(HT_OFF)
            hT = AR.take([128, 8, S], BF16)
            tabs = AR.take([128, 4, S], BF16)
            cosA, sinA, cosB, sinB = tabs[:, 0, :], tabs[:, 1, :], tabs[:, 2, :], tabs[:, 3, :]
            B_hT = Sd.bufs("hT", 4)
            B_tabs = Sd.buf("tabs")
            AR.at(GEN_OFF)
            regX = AR.take([128, 32768 // 2], BF16)
            wprep = regX[:, 0:8 * C_END].rearrange("p (c n) -> p c n", c=8)
            wqb = AR.take([128, 2, 768], BF16)
            wqbsw = AR.take([128, 2, 768], BF16)
            wkvb = AR.take([128, 1024], BF16)
            wkpe = AR.take([128, 8, 256], BF16)
            W2_END = AR.off
            B_w = Sd.buf("wprep")
            assert AR.off == XALL_OFF, AR.off
            xall = AR.take([128, NT, D], F32)
            xn = AR.take([128, NT, D], BF16)
            junk = AR.take([128, D], BF16)
            B_x = pre_x["bufs"] if pre_x else Sd.bufs("x", NT)
            B_xn = Sd.bufs("xn", NT)
            B_junk = Sd.buf("junk")
            for t in range(pre_x["n"] if pre_x else 0, NT):
                Sd.dma(xall[:, t, :], x_d[row0 + t * 128: row0 + (t + 1) * 128, :], writes=[B_x[t]])
            pre_x = None
            for dst_, src_ in ((wprep, kview(w_f32["wprep"])), (wqb, kview(w_f32["wqb"])), (wqbsw, kview(w_f32["wqbsw"])),
                               (wkvb, w_f32["wkvb"]), (wkpe, kview(w_f32["wkpe"]))):
                Sd.dma(dst_, src_, reads=[B_x[NT // 2 - 1]], writes=[B_w], q="pool", sem_buf=B_w)
            Sd.dma(tabs, tabs_d.rearrange("a p n -> p a n"), writes=[B_tabs])
            for t in range(NT):
                Sd.op("act", lambda e, t=t: e.activation(out=junk, in_=xall[:, t, :], func=AF.Square,
                                                         accum_out=stat[:, t:t + 1]),
                      reads=[B_x[t]], writes=[B_junk, B_stat])
            rstd_batch(stat[:, 0:NT], D, [B_stat], [B_stat])
            norm_transpose(xall, B_x, NT, stat, B_stat, G_MIX, hT, B_hT, xn, B_xn, [0, 1, 2, 3])

            if stop == 1:
                break
            Sd.barrier()
            AR.at(GEN_OFF)
            AR.at(W2_END)
            cqT = AR.take([128, 2, S], BF16)
            ckvT = AR.take([128, S], BF16)
            QTA = [AR.take([128, S], BF16) for _ in range(2)]
            KTA = [AR.take([128, S], BF16) for _ in range(2)]
            VA = [AR.take([128, NT, 128], BF16) for _ in range(2)]
            QTB = AR.take([128, 4, S], BF16)
            KTB = AR.take([128, S], BF16)
            VB = AR.take([128, NT, 2, 192], BF16)
            PT = [AR.take([128, 2, 512], BF16) for _ in range(3)]
            rec = [AR.take([128, 512], F32) for _ in range(4)]
            rec2 = [AR.take([128, 512], F32) for _ in range(4)]
            sq = [AR.take([128, 512], BF16) for _ in range(3)]
            vr = [AR.take([128, 512], F32) for _ in range(2)]
            t1 = [AR.take([128, 512], F32) for _ in range(2)]
            t2 = [AR.take([128, 512], F32) for _ in range(2)]
            B_cq = Sd.bufs("cq", 4)
            B_ckv = Sd.bufs("ckv", 4)
            B_QTA = [Sd.bufs("qta", 4), Sd.bufs("qta_", 4)]
            B_KTA = [Sd.bufs("kta", 4), Sd.bufs("kta_", 4)]
            B_kpe = [Sd.bufs("kpe", 4), Sd.bufs("kpe_", 4)]
            B_VA = [Sd.buf("va0"), Sd.buf("va1")]
            B_QTB = Sd.bufs("qtb", 16)
            B_KTB = Sd.bufs("ktb", 4)
            B_VB = Sd.buf("vb")
            B_PT = Sd.bufs("pt", 3)
            B_rec = Sd.bufs("rec", 4)
            B_rec2 = Sd.bufs("rec2", 4)
            B_sq = Sd.bufs("sq", 3)
            B_vr = Sd.bufs("vr", 2)
            B_t1 = Sd.bufs("t1", 2)
            B_t2 = Sd.bufs("t2", 2)

            if b == 0:
                for k in SCRATCH:
                    convert(k, after=[B_w])
            Sd.op("pool", lambda e: e.memset(VA[0][:, :, 64:128], 1.0), writes=[B_VA[0]])
            Sd.op("pool", lambda e: e.memset(VA[1][:, :, 0:64], 1.0), writes=[B_VA[1]])
            Sd.op("pool", lambda e: e.memset(VB[:, :, :, 0:64], 1.0), writes=[B_VB])
            Sd.op("pool", lambda e: e.memset(VB[:, :, :, 128:192], 1.0), writes=[B_VB])

            rr = [0]

            def nxt(n=2):
                rr[0] += 1
                return rr[0] % n

            def proj_fm(bank, wcols_fn, tb, kch=8):
                Sd.op("pe", lambda e: mm(e, ps(bank)[0:wcols_fn(0).shape[1], :],
                                         [(wcols_fn(c), hT[:, c, tb * 512:(tb + 1) * 512]) for c in range(kch)]),
                      reads=[B_hT[tb], B_w], writes=[B_ps[bank]])

            def rstd_big(iv, ssb, nfeat):
                if os.environ.get("MK_NOLN"):
                    Sd.op("dve", lambda e: e.tensor_scalar(out=vr[iv], in0=ps(ssb), scalar1=1.0 / nfeat, scalar2=EPS,
                                                           op0=ALU.mult, op1=ALU.add), reads=[B_ps[ssb]], writes=[B_vr[iv]])
                    Sd.op("act", lambda e: e.activation(out=vr[iv], in_=vr[iv], func=AF.Sqrt), reads=[B_vr[iv]], writes=[B_vr[iv]])
                    Sd.op("dve", lambda e: e.reciprocal(out=vr[iv], in_=vr[iv]), reads=[B_vr[iv]], writes=[B_vr[iv]])
                    return
                Sd.op("dve", lambda e: e.tensor_scalar(out=vr[iv], in0=ps(ssb), scalar1=1.0 / nfeat, scalar2=EPS,
                                                       op0=ALU.mult, op1=ALU.add), reads=[B_ps[ssb]], writes=[B_vr[iv]])
                Sd.op("act", lambda e: e.activation(out=vr[iv], in_=vr[iv], func=AF.Ln), reads=[B_vr[iv]], writes=[B_vr[iv]])
                Sd.op("act", lambda e: e.activation(out=vr[iv], in_=vr[iv], func=AF.Exp, scale=-0.5),
                      reads=[B_vr[iv]], writes=[B_vr[iv]])

            for tb in range(4):
                tsl = slice(tb * 512, (tb + 1) * 512)
                for j in range(3):
                    proj_fm(j, lambda c, j=j: wprep[:, c, j * 128:(j + 1) * 128], tb)
                    Sd.op("act", lambda e, j=j: e.activation(out=sq[j], in_=ps(j), func=AF.Square),
                          reads=[B_ps[j]], writes=[B_sq[j]])
                Sd.op("pe", lambda e: mm(e, ps(3), [(ones_all, sq[0]), (ones_all, sq[1])]),
                      reads=[B_sq[0], B_sq[1], B_const], writes=[B_ps[3]])
                Sd.op("pe", lambda e: mm(e, ps(4), [(ones_all, sq[2])]),
                      reads=[B_sq[2], B_const], writes=[B_ps[4]])
                rstd_big(0, 3, 256)
                rstd_big(1, 4, 128)
                for bk, dst, gc, bd, iv in ((0, cqT[:, 0, tsl], G_QA, B_cq[tb], 0), (1, cqT[:, 1, tsl], G_QA + 1, B_cq[tb], 0),
                                            (2, ckvT[:, tsl], G_KVA, B_ckv[tb], 1)):
                    Sd.op("dve", lambda e, bk=bk, dst=dst, gc=gc, iv=iv: e.scalar_tensor_tensor(
                        out=dst, in0=ps(bk), scalar=gvec[:, gc:gc + 1], in1=vr[iv], op0=ALU.mult, op1=ALU.mult),
                        reads=[B_ps[bk], B_vr[iv], B_const], writes=[bd])

            for tb in range(4):
                tsl = slice(tb * 512, (tb + 1) * 512)
                proj_fm(5, lambda c: wkpe[:, c, 0:128], tb)
                proj_fm(6, lambda c: wkpe[:, c, 128:256], tb)
                i1 = nxt()
                Sd.op("dve", lambda e, i1=i1: e.tensor_tensor(out=t1[i1][64:96, :], in0=ps(5)[64:96, :],
                                                              in1=cosA[64:96, tsl], op=ALU.mult),
                      reads=[B_ps[5], B_tabs], writes=[B_t1[i1]])
                Sd.op("dve", lambda e, i1=i1: e.tensor_tensor(out=t2[i1][64:96, :], in0=ps(6)[64:96, :],
                                                              in1=sinA[64:96, tsl], op=ALU.mult),
                      reads=[B_ps[6], B_tabs], writes=[B_t2[i1]])
                for sl in range(2):
                    Sd.op(PEL, lambda e, i1=i1, sl=sl: e.tensor_tensor(out=KTA[sl][64:96, tsl], in0=t1[i1][64:96, :],
                                                                          in1=t2[i1][64:96, :], op=ALU.add),
                          reads=[B_t1[i1], B_t2[i1]], writes=[B_kpe[sl][tb]])

            def prep_b(colA, colS, gcol, gcolS, dst_fn, bdst_fn):
                for tb in range(4):
                    tsl = slice(tb * 512, (tb + 1) * 512)
                    bA, bS, bSS = 0 + 3 * (tb % 2), 1 + 3 * (tb % 2), 2 + 3 * (tb % 2)
                    proj_fm(bA, lambda c: wprep[:, c, colA:colA + 128], tb)
                    proj_fm(bS, lambda c: wprep[:, c, colS:colS + 128], tb)
                    i = nxt()
                    Sd.op("act", lambda e, i=i, bA=bA: e.activation(out=sq[i], in_=ps(bA), func=AF.Square),
                          reads=[B_ps[bA]], writes=[B_sq[i]])
                    Sd.op("pe", lambda e, i=i, bSS=bSS: mm(e, ps(bSS), [(ones_blk, sq[i])]),
                          reads=[B_sq[i], B_const], writes=[B_ps[bSS]])
                    rstd_big(i, bSS, 64)
                    Sd.op("dve", lambda e, i=i, bA=bA: e.scalar_tensor_tensor(
                        out=t1[i], in0=ps(bA), scalar=gvec[:, gcol:gcol + 1], in1=cosB[:, tsl], op0=ALU.mult, op1=ALU.mult),
                        reads=[B_ps[bA], B_const, B_tabs], writes=[B_t1[i]])
                    Sd.op("dve", lambda e, i=i, bS=bS: e.scalar_tensor_tensor(
                        out=t2[i], in0=ps(bS), scalar=gvec[:, gcolS:gcolS + 1], in1=sinB[:, tsl], op0=ALU.mult, op1=ALU.mult),
                        reads=[B_ps[bS], B_const, B_tabs], writes=[B_t2[i]])
                    Sd.op(PEL, lambda e, i=i: e.tensor_tensor(out=t1[i], in0=t1[i], in1=t2[i], op=ALU.add),
                          reads=[B_t1[i], B_t2[i]], writes=[B_t1[i]])
                    Sd.op(PEL, lambda e, i=i, tb=tb: e.tensor_tensor(out=dst_fn(tsl), in0=t1[i], in1=vr[i], op=ALU.mult),
                          reads=[B_t1[i], B_vr[i]], writes=[bdst_fn(tb)])

            for j in range(4):
                prep_b(C_QB + j * 128, C_QBSW + j * 128, G_QN, G_QNSW,
                       lambda tsl, j=j: QTB[:, j, tsl], lambda tb, j=j: B_QTB[j * 4 + tb])
            prep_b(C_KB, C_KBSW, G_KN, G_KNSW, lambda tsl: KTB[:, tsl], lambda tb: B_KTB[tb])

            for t in range(NT):
                bk = 6 + (t % 2)
                Sd.op("pe", lambda e, t=t, bk=bk: mm(e, ps(bk)[:, 0:128],
                                                     [(hT[:, c, t * 128:(t + 1) * 128], wprep[:, c, C_VB:C_VB + 128])
                                                      for c in range(8)]),
                      reads=[B_hT[t // 4], B_w], writes=[B_ps[bk]])
                Sd.op("dve", lambda e, t=t, bk=bk: e.tensor_copy(
                    out=VB[:, t, :, 64:128], in_=ps(bk)[:, 0:128].rearrange("p (g d) -> p g d", g=2)),
                    reads=[B_ps[bk]], writes=[B_VB])

            if stop == 2:
                break
            Sd.barrier()
            oaT = regX[:, 0:4 * S].rearrange("p (c n) -> p c n", c=4)
            obT = regX[:, 4 * S:8 * S].rearrange("p (c n) -> p c n", c=4)
            B_oa = Sd.bufs("oa", 16)
            B_ob = Sd.bufs("ob", 16)

            def prep_a_pieces(h):
                sl = h % 2
                pieces = []
                for tb in range(4):
                    tsl = slice(tb * 512, (tb + 1) * 512)

                    def pa(banks, tb=tb, tsl=tsl):
                        b0, b1 = banks
                        Sd.op("pe", lambda e: mm(e, ps(b0)[0:96, :], [(wqb[:, c, h * 96:(h + 1) * 96], cqT[:, c, tsl])
                                                                      for c in range(2)]),
                              reads=[B_cq[tb], B_w], writes=[B_ps[b0]])
                        Sd.op("pe", lambda e: mm(e, ps(b1)[0:96, :], [(wqbsw[:, c, h * 96:(h + 1) * 96], cqT[:, c, tsl])
                                                                      for c in range(2)]),
                              reads=[B_cq[tb], B_w], writes=[B_ps[b1]])
                        i1 = nxt()
                        Sd.op("dve", lambda e: e.tensor_copy(out=QTA[sl][0:64, tsl], in_=ps(b0)[0:64, :]),
                              reads=[B_ps[b0]], writes=[B_QTA[sl][tb]])
                        Sd.op("dve", lambda e: e.tensor_tensor(out=t1[i1][64:96, :], in0=ps(b0)[64:96, :],
                                                               in1=cosA[64:96, tsl], op=ALU.mult),
                              reads=[B_ps[b0], B_tabs], writes=[B_t1[i1]])
                        Sd.op("dve", lambda e: e.tensor_tensor(out=t2[i1][64:96, :], in0=ps(b1)[64:96, :],
                                                               in1=sinA[64:96, tsl], op=ALU.mult),
                              reads=[B_ps[b1], B_tabs], writes=[B_t2[i1]])
                        Sd.op(PEL, lambda e: e.tensor_tensor(out=QTA[sl][64:96, tsl], in0=t1[i1][64:96, :],
                                                             in1=t2[i1][64:96, :], op=ALU.add),
                              reads=[B_t1[i1], B_t2[i1]], writes=[B_QTA[sl][tb]])
                    pieces.append(pa)
                vo = 0 if sl == 0 else 64
                for t4 in range(4):
                    tsl = slice(t4 * 512, (t4 + 1) * 512)

                    def pkv(banks, t4=t4, tsl=tsl):
                        b0, b1 = banks
                        Sd.op("pe", lambda e: mm(e, ps(b0)[0:64, :], [(wkvb[:, h * 128:h * 128 + 64], ckvT[:, tsl])]),
                              reads=[B_ckv[t4], B_w], writes=[B_ps[b0]])
                        Sd.op("dve", lambda e: e.tensor_copy(out=KTA[sl][0:64, tsl], in_=ps(b0)[0:64, :]),
                              reads=[B_ps[b0]], writes=[B_KTA[sl][t4]])

                        def vmm(e):
                            ins = None
                            for i in range(4):
                                t = t4 * 4 + i
                                ins = e.matmul(ps(b1)[:, i * 64:(i + 1) * 64], ckvT[:, t * 128:(t + 1) * 128],
                                               wkvb[:, h * 128 + 64:h * 128 + 128], start=True, stop=True)
                            return ins
                        Sd.op("pe", vmm, reads=[B_ckv[t4], B_w], writes=[B_ps[b1]])
                        Sd.op("dve", lambda e: e.tensor_copy(
                            out=VA[sl][:, t4 * 4:(t4 + 1) * 4, vo:vo + 64],
                            in_=ps(b1)[:, 0:256].rearrange("p (t d) -> p t d", t=4)),
                            reads=[B_ps[b1]], writes=[B_VA[sl]])
                    pieces.append(pkv)
                return pieces

            nrm = [0]

            def attention(steps, scale, pairs, look, fillers=None, ndummy=0):
                n = len(steps)
                fillers = fillers or {}
                free = [[p_, -1] for p_ in pairs]
                nq = [0]
                pending = []

                def qk(i, banks):
                    s_ = steps[i]
                    s_["banks"] = banks

                    def f(e):
                        ins = None
                        for jj in range(2):
                            ins = e.matmul(ps(banks[jj]), s_["qk"][jj][0], s_["qk"][jj][1], start=True, stop=True)
                        return ins
                    Sd.op("pe", f, reads=s_["rQK"], writes=[B_ps[banks[0]], B_ps[banks[1]]])
                    if ndummy:
                        def dm(e):
                            ins = None
                            for jj in range(ndummy):
                                ins = e.matmul(ps(banks[jj % 2]), zeros_w, cosB[:, 0:512], start=False, stop=(jj == ndummy - 1))
                            return ins
                        Sd.op("pe", dm, reads=[B_const, B_tabs], writes=[B_ps[banks[0]], B_ps[banks[1]]])

                def issue_qk(i, force):
                    while nq[0] < n and nq[0] <= i + look:
                        must = force and nq[0] <= i
                        cand = [f_ for f_ in free if f_[1] <= i or must]
                        if not cand:
                            assert not must
                            break
                        cand.sort(key=lambda f_: f_[1])
                        free.remove(cand[0])
                        qk(nq[0], cand[0][0])
                        nq[0] += 1

                for i in range(n):
                    s_ = steps[i]
                    issue_qk(i, True)
                    banks = s_["banks"]
                    assert banks[1] == banks[0] + 1
                    pi = i % 3
                    Sd.op("act", lambda e, banks=banks, pi=pi: e.activation(
                        out=PT[pi].rearrange("p a b -> p (a b)"), in_=ps2(banks[0]), func=AF.Exp, scale=scale),
                        reads=[B_ps[banks[0]], B_ps[banks[1]]], writes=[B_PT[pi]])

                    def pv(e, s_=s_, pi=pi):
                        ins = None
                        for jj in range(2):
                            ob, v_ap, st_, sp_ = s_["pv"][jj]
                            ins = e.matmul(ps(ob), v_ap, PT[pi][:, jj, :], start=st_, stop=sp_)
                        return ins
                    Sd.op("pe", pv, reads=[B_PT[pi]] + s_["rV"], writes=sorted(set(B_ps[o_[0]] for o_ in s_["pv"]), key=id))
                    for ob, par, dst, bdst in s_.get("fin", ()):
                        orow = slice(par * 64, par * 64 + 64)
                        drow = slice((1 - par) * 64, (1 - par) * 64 + 64)
                        ri = nrm[0] % 4
                        nrm[0] += 1
                        Sd.op("dve", lambda e, ob=ob, ri=ri, drow=drow: e.reciprocal(out=rec[ri][drow, :], in_=ps(ob)[drow, :]),
                              reads=[B_ps[ob]], writes=[B_rec[ri]])
                        Sd.dma(rec2[ri][orow, :], rec[ri][drow, :], reads=[B_rec[ri]], writes=[B_rec2[ri]])

                        def fin_mult(ob=ob, ri=ri, orow=orow, dst=dst, bdst=bdst):
                            Sd.op("dve", lambda e: e.tensor_tensor(
                                out=dst[orow, :], in0=ps(ob)[orow, :], in1=rec2[ri][orow, :], op=ALU.mult),
                                reads=[B_ps[ob], B_rec2[ri]], writes=[bdst])
                        pending.append((i + 5, fin_mult))
                    free.append([banks, i])
                    for f_ in fillers.get(i, ()):
                        cand = sorted(free, key=lambda f__: f__[1])
                        if cand:
                            free.remove(cand[0])
                            f_(cand[0][0])
                            free.append([cand[0][0], i + 3])
                        else:
                            fillers.setdefault(i + 1, []).append(f_)
                    for it_ in [p_ for p_ in pending if p_[0] <= i]:
                        pending.remove(it_)
                        it_[1]()
                for it_ in pending:
                    it_[1]()

            pairsA = [(0, 1), (2, 3), (4, 5)]
            for k_, pc in enumerate(prep_a_pieces(0)):
                pc(pairsA[k_ % 3])
            steps = []
            fillers = {}
            unit = 0
            for h in range(8):
                sl = h % 2
                base = len(steps)
                if h + 1 < 8:
                    pcs = prep_a_pieces(h + 1)
                    order = [pcs[0], pcs[4], pcs[5], pcs[6], pcs[7], pcs[1], pcs[2], pcs[3]]
                    for off_, pc in zip((2, 5, 10, 13, 18, 21, 26, 29), order):
                        fillers.setdefault(base + off_, []).append(pc)
                for qb_ in range(4):
                    ob = 6 + (unit % 2)
                    for kp in range(8):
                        st_ = dict(
                            qk=[(KTA[sl][0:96, (kp * 2 + jj) * 128:(kp * 2 + jj + 1) * 128],
                                 QTA[sl][0:96, qb_ * 512:(qb_ + 1) * 512]) for jj in range(2)],
                            rQK=[B_QTA[sl][qb_], B_KTA[sl][kp // 2], B_kpe[sl][kp // 2]],
                            pv=[(ob, VA[sl][:, kp * 2 + jj, :], kp == 0 and jj == 0, kp == 7 and jj == 1) for jj in range(2)],
                            rV=[B_VA[sl]])
                        if kp == 7:
                            st_["fin"] = [(ob, h % 2, oaT[:, h // 2, qb_ * 512:(qb_ + 1) * 512], B_oa[(h // 2) * 4 + qb_])]
                        steps.append(st_)
                    unit += 1
            attention(steps, SCALE_A, pairsA, LOOKA, fillers, ndummy=NDUMMY_A)
            AR.at(GEN_OFF + 32 * KB, MERGED_OFF)
            wg = [AR.take([128, 8, 2, 128], BF16) for _ in range(2)]
            woab = [AR.take([128, 4, 2, 128], BF16) for _ in range(2)]
            B_wg = Sd.bufs("wg", 2)
            B_woab = Sd.bufs("woab", 2)
            wgate_v = w_bf["wgate"].rearrange("(c p) (g n) -> p c g n", p=128, g=2)

            def load_merge_w(c, extra=()):
                wi = c % 2
                for g_ in range(2):
                    Sd.dma(wg[wi][:, :, g_, :], wgate_v[:, :, g_, c * 128:(c + 1) * 128], reads=[B_scr["wgate"]],
                           writes=[B_wg[wi]] + (list(extra) if g_ == 0 else []), sem_buf=B_wg[wi])
                Sd.dma(woab[wi][:, :, 0, :], w_bf["woa"].rearrange("(j p) n -> p j n", p=128)[:, :, c * 128:(c + 1) * 128],
                       reads=[B_scr["woa"]], writes=[B_woab[wi]], sem_buf=B_woab[wi])
                Sd.dma(woab[wi][:, :, 1, :], w_bf["wob"].rearrange("(j p) n -> p j n", p=128)[:, :, c * 128:(c + 1) * 128],
                       reads=[B_scr["wob"]], writes=[B_woab[wi]], sem_buf=B_woab[wi])

            if not BOLD:
                load_merge_w(0, extra=[B_w])
                load_merge_w(1)
            if BOLD:
                steps = []
                unit = 0
                for h in range(8):
                    g, j = h // 4, h % 4
                    voff = 64 if h % 2 == 0 else 0
                    for qb_ in range(4):
                        ob = 6 + (unit % 2)
                        qsl = slice(qb_ * 512, (qb_ + 1) * 512)
                        for kp in range(8):
                            st_ = dict(
                                qk=[(KTB[g * 64:(g + 1) * 64, (kp * 2 + jj) * 128:(kp * 2 + jj + 1) * 128],
                                     QTB[g * 64:(g + 1) * 64, j, qsl]) for jj in range(2)],
                                rQK=[B_QTB[j * 4 + qb_], B_KTB[kp // 2]],
                                pv=[(ob, VB[:, kp * 2 + jj, g, voff:voff + 128], kp == 0 and jj == 0, kp == 7 and jj == 1) for jj in range(2)],
                                rV=[B_VB])
                            if kp == 7:
                                st_["fin"] = [(ob, h % 2, obT[:, h // 2, qsl], B_ob[(h // 2) * 4 + qb_])]
                            steps.append(st_)
                        unit += 1
                attention(steps, SCALE_B, pairsA, 2)
            else:
                steps = []
                unit = 0
                for j in range(4):
                    for qb_ in range(4):
                        obs = (4 + 2 * (unit % 2), 5 + 2 * (unit % 2))
                        qsl = slice(qb_ * 512, (qb_ + 1) * 512)
                        voff = 64 if j % 2 == 0 else 0
                        for kt in range(NT):
                            ksl = slice(kt * 128, (kt + 1) * 128)
                            st_ = dict(
                                qk=[(KTB[g * 64:(g + 1) * 64, ksl], QTB[g * 64:(g + 1) * 64, j, qsl]) for g in range(2)],
                                rQK=[B_QTB[j * 4 + qb_], B_KTB[kt // 4]],
                                pv=[(obs[g], VB[:, kt, g, voff:voff + 128], kt == 0, kt == NT - 1) for g in range(2)],
                                rV=[B_VB])
                            if kt == NT - 1:
                                st_["fin"] = [(obs[g], (j + 4 * g) % 2, obT[:, (j + 4 * g) // 2, qsl], B_ob[((j + 4 * g) // 2) * 4 + qb_])
                                              for g in range(2)]
                            steps.append(st_)
                        unit += 1
                attention(steps, SCALE_B, [(0, 1), (2, 3)], 1)

            if stop == 3:
                break
            Sd.barrier()
            WDN_OFF = 108 * KB
            AR.at(WDN_OFF)
            wdn = [AR.take([128, 8, D], BF16) for _ in range(2)]
            B_wdn = Sd.bufs("wdn", 2)
            AR.at(MERGED_OFF)
            mergedT = AR.take([128, 8, S], BF16)
            AR.at(GEN_OFF + 32 * KB + 12 * KB, MERGED_OFF)
            tg = [AR.take([128, 512], F32) for _ in range(4)]
            mm_ = [AR.take([128, 512], F32) for _ in range(4)]
            B_tg = Sd.bufs("tg", 4)
            B_mm = Sd.bufs("mm", 4)
            B_mg = Sd.bufs("mg", 4)
            for c in range(8):
                wi = c % 2
                if c >= 2 or BOLD:
                    load_merge_w(c)
                if c == 1:
                    wload(wdn[1], "wo", kview(w_bf["wo"]), B_wdn[1])
                    wload(wdn[0], "wdown", kview(w_bf["wdown"][0:1024, :]), B_wdn[0])
                for tb in range(4):
                    tsl = slice(tb * 512, (tb + 1) * 512)
                    pb = 4 * (tb % 2)
                    for g_ in range(2):
                        Sd.op("pe", lambda e, g_=g_, pb=pb: mm(e, ps(pb + g_), [(wg[wi][:, k, g_, :], hT[:, k, tsl]) for k in range(8)]),
                              reads=[B_hT[tb], B_wg[wi]], writes=[B_ps[pb + g_]])
                    Sd.op("pe", lambda e, pb=pb: mm(e, ps(pb + 2), [(woab[wi][:, j, 0, :], oaT[:, j, tsl]) for j in range(4)]),
                          reads=[B_oa[j * 4 + tb] for j in range(4)] + [B_woab[wi]], writes=[B_ps[pb + 2]])
                    Sd.op("pe", lambda e, pb=pb: mm(e, ps(pb + 3), [(woab[wi][:, j, 1, :], obT[:, j, tsl]) for j in range(4)]),
                          reads=[B_ob[j * 4 + tb] for j in range(4)] + [B_woab[wi]], writes=[B_ps[pb + 3]])
                    ia, ib = 2 * (tb % 2), 2 * (tb % 2) + 1
                    for g_, ii in ((0, ia), (1, ib)):
                        Sd.op("act", lambda e, g_=g_, ii=ii, pb=pb: e.activation(out=tg[ii], in_=ps(pb + g_), func=AF.Tanh, scale=0.5),
                              reads=[B_ps[pb + g_]], writes=[B_tg[ii]])
                        Sd.op("dve", lambda e, g_=g_, ii=ii, pb=pb: e.scalar_tensor_tensor(
                            out=mm_[ii], in0=tg[ii], scalar=1.0, in1=ps(pb + 2 + g_), op0=ALU.add, op1=ALU.mult),
                            reads=[B_tg[ii], B_ps[pb + 2 + g_]], writes=[B_mm[ii]])
                    Sd.op(PEL, lambda e, ia=ia, ib=ib: e.tensor_tensor(out=mergedT[:, c, tsl], in0=mm_[ia], in1=mm_[ib], op=ALU.add),
                          reads=[B_mm[ia], B_mm[ib]], writes=[B_mg[tb]])

            if stop == 4:
                break
            Sd.barrier()
            NB = 8
            AR.at(0, MERGED_OFF)
            x1 = AR.take([128, NB, D], F32)
            xt = [AR.take([128, D], F32) for _ in range(2)]
            h2T = AR.take([128, 8, NB * 128], BF16)
            uT = AR.take([128, 8, NB * 128], BF16)
            xn5 = uT.rearrange("p a b -> p (a b)").rearrange("p (t n) -> p t n", t=NB)
            wA = AR.take([128, 10, D], BF16)
            wup = [AR.take([128, 8, 512], BF16) for _ in range(2)]
            assert AR.off == WDN_OFF, AR.off
            AR.off += 2 * 16 * KB
            rl = [AR.take([128, 512], F32) for _ in range(2)]
            pbf = AR.take([128, NB, 256], BF16)
            pT = AR.take([128, 2, NB * 128], BF16)
            tgp = AR.take([128, D], F32)
            mp = AR.take([128, D], F32)
            junk5 = AR.take([128, D], BF16)
            ost2 = AR.take([128, D], F32)
            B_x1 = Sd.bufs("x1", NB)
            B_xt = Sd.bufs("xt", 2)
            B_h2 = Sd.bufs("h2T", NB // 4)
            B_uT = Sd.buf("uT")
            B_xn5 = [B_uT] * NB
            B_wA = Sd.buf("wA")
            B_wup = Sd.bufs("wup", 2)
            B_rl = Sd.bufs("rl", 2)
            B_pbf = Sd.buf("pbf")
            B_pT = Sd.buf("pT")
            B_tgp = Sd.buf("tgp")
            B_mp = Sd.buf("mp")
            ost = [tgp, mp, ost2]
            B_ost = [B_tgp, B_mp, Sd.buf("ost2")]
            B_junk5 = Sd.buf("junk5")
            B_stat2 = Sd.buf("stat2")
            B_stat3 = Sd.buf("stat3")
            wo_v = wdn[1]
            wpg_v = wA[:, 0:8, :]
            wple_v = wA[:, 8:10, :]
            wup_v = kview(w_bf["wup"])
            NBLK = NT // NB

            def cast_tile(t):
                if t % 2 == 0:
                    Sd.op("act", lambda e: e.activation(out=xn5[:, t, :], in_=x1[:, t, :], func=AF.Identity),
                          reads=[B_x1[t]], writes=[B_xn5[t]])
                else:
                    Sd.op("dve", lambda e: e.tensor_copy(out=xn5[:, t, :], in_=x1[:, t, :]),
                          reads=[B_x1[t]], writes=[B_xn5[t]])

            def statsN(col0):
                for t in range(NB):
                    Sd.op("act", lambda e, t=t: e.activation(out=junk5, in_=x1[:, t, :], func=AF.Square,
                                                             accum_out=stat[:, col0 + t:col0 + t + 1]),
                          reads=[B_x1[t]], writes=[B_junk5, B_stat])
                rstd_batch(stat[:, col0:col0 + NB], D, [B_stat], [B_stat])

            def load_up(blk_, q_, g_):
                wi = g_ % 2
                f0 = q_ * 1024 + g_ * 512
                wload(wup[wi], "wup", wup_v[:, :, f0:f0 + 512], B_wup[wi])

            def load_dn(q_):
                wload(wdn[q_ % 2], "wdown", kview(w_bf["wdown"][q_ * 1024:(q_ + 1) * 1024, :]), B_wdn[q_ % 2])

            wload(wpg_v, "wpg", kview(w_bf["wpg"]), B_wA)
            wload(wple_v, "wple", kview(w_bf["wple"]), B_wA)
            for blk in range(NBLK):
                r0 = row0 + blk * NB * 128
                tok0 = blk * NB * 128
                load_up(blk, 0, 0)
                load_up(blk, 0, 1)
                if blk > 0:
                    load_dn(0)
                for t in range(NB):
                    xi = t % 2
                    Sd.dma(xt[xi], x_d[r0 + t * 128:r0 + (t + 1) * 128, :], writes=[B_xt[xi]])
                    pb = 2 * (t % 4)

                    def omm(e, t=t, pb=pb):
                        ins = None
                        for cb in range(2):
                            ins = mm(e, ps(pb + cb), [(mergedT[:, k, tok0 + t * 128: tok0 + (t + 1) * 128],
                                                       wo_v[:, k, cb * 512:(cb + 1) * 512]) for k in range(8)])
                        return ins
                    Sd.op("pe", omm, reads=[B_mg[(tok0 + t * 128) // 512], B_wdn[1]], writes=[B_ps[pb], B_ps[pb + 1]])
                    Sd.op("dve", lambda e, t=t, pb=pb, xi=xi: e.scalar_tensor_tensor(
                        out=x1[:, t, :], in0=ps2(pb), scalar=0.5, in1=xt[xi], op0=ALU.mult, op1=ALU.add),
                        reads=[B_ps[pb], B_ps[pb + 1], B_xt[xi]], writes=[B_x1[t]])
                    cast_tile(t)
                Sd.dma(pbf, p_d[r0:r0 + NB * 128, :].rearrange("(t p) n -> p t n", p=128), writes=[B_pbf], q="pool")
                norm_transpose(x1, B_x1, NB, None, B_stat, G_MLP, h2T, B_h2, xn5, B_xn5, [4, 5, 6, 7])
                statsN(16)
                Sd.op("dve", lambda e: e.tensor_tensor(out=stat[:, 40:40 + NB], in0=stat[:, 16:16 + NB], in1=stat[:, 16:16 + NB], op=ALU.mult),
                      reads=[B_stat], writes=[B_stat2])
                for q_ in range(4):
                    for g_ in range(2):
                        wi = g_ % 2
                        for fi in range(4):
                            fc = g_ * 4 + fi
                            for th in range(NB // 4):
                                bk = (fi * 2 + th) % 4
                                Sd.op("pe", lambda e, fi=fi, bk=bk, th=th: mm(e, ps(bk), [
                                    (wup[wi][:, k, fi * 128:(fi + 1) * 128], h2T[:, k, th * 512:(th + 1) * 512]) for k in range(8)]),
                                    reads=[B_h2[th], B_wup[wi]], writes=[B_ps[bk]])
                                ri = (fi * 2 + th) % 2
                                Sd.op("act", lambda e, bk=bk, ri=ri: e.activation(out=rl[ri], in_=ps(bk), func=AF.Relu),
                                      reads=[B_ps[bk]], writes=[B_rl[ri]])
                                Sd.op("dve", lambda e, fc=fc, ri=ri, th=th: e.tensor_tensor(
                                    out=uT[:, fc, th * 512:(th + 1) * 512], in0=rl[ri], in1=rl[ri], op=ALU.mult),
                                    reads=[B_rl[ri]], writes=[B_uT])
                        nq, ng = (q_, g_ + 2) if g_ + 2 < 2 else (q_ + 1, g_)
                        if nq < 4:
                            load_up(blk, nq, ng)
                    for t in range(NB):
                        pb = 4 + 2 * (t % 2)

                        def dmm(e, t=t, pb=pb):
                            ins = None
                            for cb in range(2):
                                ins = mm(e, ps(pb + cb), [(uT[:, fc, t * 128:(t + 1) * 128], wdn[q_ % 2][:, fc, cb * 512:(cb + 1) * 512])
                                                          for fc in range(8)])
                            return ins
                        Sd.op("pe", dmm, reads=[B_uT, B_wdn[q_ % 2]], writes=[B_ps[pb], B_ps[pb + 1]])
                        Sd.op("dve", lambda e, t=t, pb=pb: e.scalar_tensor_tensor(
                            out=x1[:, t, :], in0=ps2(pb), scalar=stat[:, 40 + t:41 + t], in1=x1[:, t, :], op0=ALU.mult, op1=ALU.add),
                            reads=[B_ps[pb], B_ps[pb + 1], B_x1[t], B_stat2], writes=[B_x1[t]])

                        if t == 0 and q_ + 1 < 4:
                            load_dn(q_ + 1)
                if blk + 1 < NBLK:
                    wload(wo_v, "wo", kview(w_bf["wo"]), B_wdn[1])
                elif b + 1 < NSEQ:
                    save_off, save_lim = AR.off, AR.lim
                    AR.at(XALL_OFF)
                    xall_nx = AR.take([128, NT, D], F32)
                    AR.at(save_off, save_lim)
                    nb_ = Sd.bufs("x", NT)
                    olds = [B_wup[0]] * 2 + [B_wup[1]] * 2 + [B_wdn[0]] * 4 + [B_wdn[1]] * 4
                    for t in range(12):
                        Sd.dma(xall_nx[:, t, :], x_d[(b + 1) * S + t * 128:(b + 1) * S + (t + 1) * 128, :],
                               writes=[nb_[t], olds[t]], sem_buf=nb_[t])
                    pre_x = dict(bufs=nb_, n=12)
                for t in range(NB):
                    cast_tile(t)
                norm_transpose(x1, B_x1, NB, None, B_stat, G_PLE, h2T, B_h2, xn5, B_xn5, [4, 5, 6, 7])
                statsN(24)
                Sd.op("dve", lambda e: e.tensor_scalar(out=stat[:, 48:48 + NB], in0=stat[:, 24:24 + NB], scalar1=0.5, scalar2=None, op0=ALU.mult),
                      reads=[B_stat], writes=[B_stat3])
                for j in range(2):
                    def ptr(e, j=j):
                        ins = None
                        for t in range(NB):
                            ins = e.transpose(out=ps_bf(j)[:, t * 128:(t + 1) * 128],
                                              in_=pbf[:, t, j * 128:(j + 1) * 128], identity=ident)
                        return ins
                    Sd.op("pe", ptr, reads=[B_pbf, B_const], writes=[B_ps[j]])
                    Sd.op("dve", lambda e, j=j: e.tensor_copy(out=pT[:, j, :], in_=ps_bf(j)), reads=[B_ps[j]], writes=[B_pT])
                for t in range(NB):
                    pb = 4 * (t % 2)

                    def gmm(e, t=t, pb=pb):
                        ins = None
                        for cb in range(2):
                            ins = mm(e, ps(pb + cb), [(h2T[:, k, t * 128:(t + 1) * 128], wpg_v[:, k, cb * 512:(cb + 1) * 512])
                                                      for k in range(8)])
                        for cb in range(2):
                            ins = mm(e, ps(pb + 2 + cb), [(pT[:, j, t * 128:(t + 1) * 128], wple_v[:, j, cb * 512:(cb + 1) * 512])
                                                          for j in range(2)])
                        return ins
                    Sd.op("pe", gmm, reads=[B_h2[t // 4], B_pT, B_wA], writes=[B_ps[pb + i] for i in range(4)])
                    Sd.op("act", lambda e, pb=pb, t=t: e.activation(out=tgp, in_=ps2(pb), func=AF.Tanh, scale=stat[:, 48 + t:49 + t]),
                          reads=[B_ps[pb], B_ps[pb + 1], B_stat3], writes=[B_tgp])
                    Sd.op("dve", lambda e, pb=pb: e.scalar_tensor_tensor(
                        out=mp, in0=tgp, scalar=1.0, in1=ps2(pb + 2), op0=ALU.add, op1=ALU.mult),
                        reads=[B_tgp, B_ps[pb + 2], B_ps[pb + 3]], writes=[B_mp])
                    Sd.op("dve", lambda e, t=t: e.scalar_tensor_tensor(
                        out=x1[:, t, :], in0=mp, scalar=0.5, in1=x1[:, t, :], op0=ALU.mult, op1=ALU.add),
                        reads=[B_mp, B_x1[t]], writes=[B_x1[t]])
                statsN(32)
                for t in range(NB):
                    oi = t % 3
                    Sd.op("dve", lambda e, t=t, oi=oi: e.scalar_tensor_tensor(
                        out=ost[oi], in0=x1[:, t, :], scalar=stat[:, 32 + t:33 + t], in1=gfin[:], op0=ALU.mult, op1=ALU.mult),
                        reads=[B_x1[t], B_stat, B_const], writes=[B_ost[oi]])
                    Sd.dma(out_d[r0 + t * 128:r0 + (t + 1) * 128, :], ost[oi], reads=[B_ost[oi]], q="pool")
        Sd.barrier()
    return nc


def _swap_idx(n, half):
    idx = np.arange(n)
    blk = idx // (2 * half)
    r = idx % (2 * half)
    return blk * 2 * half + np.where(r < half, r + half, r - half)


def _tables():
    inv = 10000.0 ** (-np.arange(0, 32, 2, dtype=np.float32) / 32.0)
    t = np.arange(S)
    cosA = np.zeros((128, S), np.float32)
    sinA = np.zeros((128, S), np.float32)
    for i in range(32):
        ang = t.astype(np.float32) * inv[i % 16]
        cosA[64 + i] = np.cos(ang)
        sinA[64 + i] = np.sin(ang) * (-1.0 if i < 16 else 1.0)
    cosB = np.zeros((128, S), np.float32)
    sinB = np.zeros((128, S), np.float32)
    row = (t // 64).astype(np.float32)
    col = (t % 64).astype(np.float32)
    for p in range(128):
        d = p % 64
        pos = row if d < 32 else col
        ang = pos * inv[d % 16]
        cosB[p] = np.cos(ang)
        sinB[p] = np.sin(ang) * (-1.0 if (d % 32) < 16 else 1.0)
    return np.stack([cosA, sinA, cosB, sinB]).astype(ml_dtypes.bfloat16)


def _consts():
    ident = np.eye(128, dtype=np.float32)
    blk = np.zeros((128, 128), np.float32)
    blk[0:64, 0:64] = 1.0
    blk[64:128, 64:128] = 1.0
    allo = np.ones((128, 128), np.float32)
    return np.stack([ident, blk, allo, np.zeros((128, 128), np.float32)]).astype(ml_dtypes.bfloat16)


def _prep_weights(inp):
    w_in = np.asarray(inp["w_in"])[0]
    q_lat = w_in[:, 0:256]
    c_kv = w_in[:, 256:384]
    k_pe = w_in[:, 384:416]
    qb = w_in[:, 416:928]
    kb = w_in[:, 928:1056]
    vb = w_in[:, 1056:1184]
    order = np.array([0, 4, 1, 5, 2, 6, 3, 7])
    qbP = qb.reshape(D, 8, 64)[:, order, :]
    sw64 = _swap_idx(64, 16)
    qbP_sw = qbP[:, :, sw64].reshape(D, 512)
    qbP = qbP.reshape(D, 512)
    kb_sw = kb.reshape(D, 2, 64)[:, :, sw64].reshape(D, 128)
    wprep = np.concatenate([q_lat, c_kv, qbP, kb, vb, qbP_sw, kb_sw], axis=1)
    assert wprep.shape[1] == C_END
    sw32 = _swap_idx(32, 16)
    wkpe = np.zeros((D, 256), np.float32)
    wkpe[:, 64:96] = k_pe
    wkpe[:, 128 + 64:128 + 96] = k_pe[:, sw32]
    wgate = w_in[:, 1184:3232]
    w_qb = np.asarray(inp["w_qb"])[0]
    wq3 = w_qb.reshape(256, 8, 96).copy()
    wq3[:, :, 64:96] = wq3[:, :, 64:96][:, :, sw32]
    w_qb_sw = wq3.reshape(256, 768)

    def gcols(v):
        return np.asarray(v, np.float32).reshape(-1, 128).T

    g_qn = np.asarray(inp["g_qn"])[0]
    g_kn = np.asarray(inp["g_kn"])[0]
    gvec = np.zeros((128, 32), np.float32)
    gvec[:, G_MIX:G_MIX + 8] = gcols(inp["g_mix"][0])
    gvec[:, G_MLP:G_MLP + 8] = gcols(inp["g_mlp"][0])
    gvec[:, G_PLE:G_PLE + 8] = gcols(inp["g_ple"][0])
    gvec[:, G_QA:G_QA + 2] = gcols(inp["g_qa"][0])
    gvec[:, G_KVA] = np.asarray(inp["g_kva"])[0]
    gvec[:, G_QN] = np.tile(g_qn, 2)
    gvec[:, G_QNSW] = np.tile(g_qn[sw64], 2)
    gvec[:, G_KN] = np.tile(g_kn, 2)
    gvec[:, G_KNSW] = np.tile(g_kn[sw64], 2)
    gfin = np.ascontiguousarray(np.broadcast_to(np.asarray(inp["g_final"], np.float32)[None, :], (128, D)))
    c = np.ascontiguousarray
    return dict(
        wprep=c(wprep), wkpe=wkpe, wgate=c(wgate), wqb=c(w_qb), wqbsw=c(w_qb_sw),
        wkvb=c(np.asarray(inp["w_kvb"])[0]), woa=c(np.asarray(inp["w_oa"])[0]), wob=c(np.asarray(inp["w_ob"])[0]),
        wo=c(np.asarray(inp["w_o"])[0]), wup=c(np.asarray(inp["w_up"])[0]), wdown=c(np.asarray(inp["w_down"])[0]),
        wpg=c(np.asarray(inp["w_ple_gate"])[0]), wple=c(np.asarray(inp["w_ple"])[0]),
        gvec=gvec, gfin=gfin, tabs=_tables(), cst=_consts())


def kernel(**inputs):
    x = np.asarray(inputs["x"], np.float32)
    p = np.asarray(inputs["p"], np.float32)[0]
    shared = _prep_weights(inputs)
    nc = build_program()
    in_maps = []
    for c in range(N_CORES):
        m = dict(shared)
        m["x"] = np.ascontiguousarray(x[2 * c:2 * c + 2].reshape(NSEQ * S, D))
        m["p"] = np.ascontiguousarray(p[2 * c:2 * c + 2].reshape(NSEQ * S, 256))
        in_maps.append(m)
    res = run_bass_kernel_spmd(nc, in_maps, core_ids=list(range(N_CORES)))
    out = np.concatenate([np.asarray(r["out"], np.float32).reshape(NSEQ, S, D) for r in res.results], axis=0)
    return out
```

```python
import os
import numpy as np
import ml_dtypes
from contextlib import ExitStack
import concourse.bass as bass
import concourse.mybir as mybir
from concourse.bass_utils import run_bass_kernel_spmd

F32 = mybir.dt.float32
BF16 = mybir.dt.bfloat16
AF = mybir.ActivationFunctionType
ALU = mybir.AluOpType

N_CORES = 8
LOOKA = 2
NDUMMY_A = 0
BOLD = False
S = 2048
D = 1024
NT = S // 128
NSEQ = 2
EPS = 1e-6
SCALE_A = 96.0 ** -0.5
SCALE_B = 64.0 ** -0.5
ENGS = ("pe", "act", "dve", "pool", "sp")

C_QLAT, C_CKV, C_QB, C_KB, C_VB, C_QBSW, C_KBSW, C_END = 0, 256, 384, 896, 1024, 1152, 1664, 1792
G_MIX, G_MLP, G_PLE, G_QA, G_KVA, G_QN, G_QNSW, G_KN, G_KNSW = 0, 8, 16, 24, 26, 27, 28, 29, 30


class Buf:
    __slots__ = ("name", "last_w", "readers", "dsem", "dcnt")

    def __init__(self, name):
        self.name = name
        self.last_w = None
        self.readers = []
        self.dsem = None
        self.dcnt = 0


class Sched:
    def __init__(self, nc, stack):
        self.nc = nc
        self.stack = stack
        self.eng = {"pe": nc.tensor, "act": nc.scalar, "dve": nc.vector, "pool": nc.gpsimd, "sp": nc.sync}
        self.sem = {e: stack.enter_context(nc.semaphore("prog_" + e)) for e in ENGS}
        self.cnt = {e: 0 for e in ENGS}
        self.seen = {e: {} for e in ENGS}
        self.nsem = 0
        self.dbufs = []

    def buf(self, name):
        return Buf(name)

    def bufs(self, name, n):
        return [Buf("%s%d" % (name, i)) for i in range(n)]

    def _dep_key(self, dep):
        if dep[0] == "e":
            return ("e", dep[1]), self.sem[dep[1]], dep[2]
        return ("d", id(dep[1])), dep[1], dep[2]

    def _wait_all(self, eng, deps):
        need = {}
        for dep in deps:
            if dep is None:
                continue
            if dep[0] == "e" and dep[1] == eng and eng == "pe":
                continue
            key, sem, val = self._dep_key(dep)
            if self.seen[eng].get(key, 0) >= val:
                continue
            if key not in need or need[key][1] < val:
                need[key] = (sem, val)
        for key, (sem, val) in need.items():
            self.eng[eng].wait_ge(sem, val)
            self.seen[eng][key] = val

    @staticmethod
    def _deps(reads, writes):
        deps = []
        for b in reads:
            deps.append(b.last_w)
        for b in writes:
            deps.append(b.last_w)
            deps.extend(b.readers)
        return deps

    def op(self, eng, fn, reads=(), writes=()):
        self._wait_all(eng, self._deps(reads, writes))
        inst = fn(self.eng[eng])
        self.cnt[eng] += 1
        inst.then_inc(self.sem[eng], 1)
        tag = ("e", eng, self.cnt[eng])
        for b in reads:
            b.readers = [r for r in b.readers if not (r[0] == "e" and r[1] == eng)]
            b.readers.append(tag)
        for b in writes:
            b.last_w = tag
            b.readers = []
        return inst

    def dma(self, out_ap, in_ap, reads=(), writes=(), q="sp", sem_buf=None):
        self._wait_all(q, self._deps(reads, writes))
        sb = sem_buf if sem_buf is not None else (writes[0] if writes else reads[0])
        if sb.dsem is None:
            sb.dsem = self.stack.enter_context(self.nc.semaphore("dma%d" % self.nsem))
            self.nsem += 1
            self.dbufs.append(sb)
        inst = self.eng[q].dma_start(out=out_ap, in_=in_ap)
        sb.dcnt += 16
        inst.then_inc(sb.dsem, 16)
        tag = ("d", sb.dsem, sb.dcnt)
        for b in reads:
            b.readers.append(tag)
        for b in writes:
            b.last_w = tag
            b.readers = []
        return inst

    def barrier(self):
        deps = [("e", e, self.cnt[e]) for e in ENGS if self.cnt[e] > 0]
        deps += [("d", b.dsem, b.dcnt) for b in self.dbufs if b.dcnt > 0 and not b.name.startswith("scr_")]
        for e in ENGS:
            self._wait_all(e, deps)


def build_program(stop=None):
    nc = bass.Bass("TRN2", target_bir_lowering=False)

    def din(name, shape, dt=F32):
        return nc.dram_tensor(name, shape, dt, kind="ExternalInput").ap()

    x_d = din("x", [NSEQ * S, D])
    p_d = din("p", [NSEQ * S, 256])
    WSHAPES = dict(wprep=[D, C_END], wkpe=[D, 256], wgate=[D, 2048], wqb=[256, 768], wqbsw=[256, 768],
                   wkvb=[128, 1024], woa=[512, D], wob=[512, D], wo=[D, D], wup=[D, 4096], wdown=[4096, D],
                   wpg=[D, D], wple=[256, D])
    WORDER = ["wprep", "wkpe", "wqb", "wqbsw", "wkvb", "wgate", "woa", "wob", "wo", "wpg", "wple", "wup", "wdown"]
    w_f32 = {k: din(k, WSHAPES[k]) for k in WORDER}
    w_bf = {k: nc.dram_tensor(k + "_bf", WSHAPES[k], BF16, kind="Internal").ap() for k in ("wgate", "woa", "wob", "wo", "wpg", "wple", "wup", "wdown")}
    gvec_d = din("gvec", [128, 32])
    gfin_d = din("gfin", [128, D])
    tabs_d = din("tabs", [4, 128, S], BF16)
    cst_d = din("cst", [4, 128, 128], BF16)
    out_d = nc.dram_tensor("out", [NSEQ * S, D], F32, kind="ExternalOutput").ap()

    with ExitStack() as st:
        Sd = Sched(nc, st)

        def sb(name, shape, dt):
            return st.enter_context(nc.sbuf_tensor("sb_" + name, shape, dt))

        cst = sb("cst", [128, 4, 128], BF16)
        ident, ones_blk, ones_all, zeros_w = cst[:, 0, :], cst[:, 1, :], cst[:, 2, :], cst[:, 3, :]
        gvec = sb("gvec", [128, 32], F32)
        gfin = sb("gfin", [128, D], F32)
        stat = sb("stat", [128, 64], F32)
        KB = 1024
        ARENA_BYTES = 200 * KB
        HT_OFF, TABS_OFF, GEN_OFF, MERGED_OFF = 0, 32 * KB, 48 * KB, 168 * KB
        arena = sb("arena", [128, ARENA_BYTES // 2], BF16)
        PS = st.enter_context(nc.psum_tensor("PS", [128, 8, 512], F32))

        B_const = Sd.buf("const")
        B_stat = Sd.buf("stat")
        B_ps = Sd.bufs("ps", 8)
        B_scr = {k: Sd.buf("scr_" + k) for k in WORDER}

        class Arena:
            def __init__(self):
                self.off = 0
                self.lim = ARENA_BYTES

            def at(self, off, lim=ARENA_BYTES):
                self.off = off
                self.lim = lim

            def take(self, shape, dt):
                n = 1
                for s_ in shape[1:]:
                    n *= s_
                nb = n * (4 if dt == F32 else 2)
                nb = (nb + 63) // 64 * 64
                assert self.off + nb <= self.lim, (self.off, nb, self.lim)
                v = arena[:, self.off // 2:(self.off + nb) // 2]
                self.off += nb
                if dt == F32:
                    v = v.bitcast(F32)
                v = v[:, 0:n]
                if len(shape) == 3:
                    v = v.rearrange("p (a b) -> p a b", a=shape[1])
                elif len(shape) == 4:
                    v = v.rearrange("p (a b c) -> p a b c", a=shape[1], b=shape[2])
                return v

        AR = Arena()
        PEL = "dve" if os.environ.get("MK_NOPOOL") else "pool"

        def ps(b):
            return PS[:, b, :]

        def ps2(b):
            return PS[:, b:b + 2, :].rearrange("p a b -> p (a b)")

        def ps_bf(b):
            return PS[:, b, :].bitcast(BF16)

        def mm(e, out, terms):
            n = len(terms)
            ins = None
            for i, (l, r) in enumerate(terms):
                ins = e.matmul(out, l, r, start=(i == 0), stop=(i == n - 1))
            return ins

        def kview(ap):
            return ap.rearrange("(c p) n -> p c n", p=128)

        Sd.dma(cst[:], cst_d.rearrange("a p n -> p a n"), writes=[B_const])
        Sd.dma(gvec[:], gvec_d, writes=[B_const])
        Sd.dma(gfin[:], gfin_d, writes=[B_const])
        def convert(k, after=()):
            rows = WSHAPES[k][0]
            nsplit = 4 if rows * WSHAPES[k][1] >= 4 * 1024 * 1024 else 1
            step = rows // nsplit
            for i in range(nsplit):
                Sd.dma(w_bf[k][i * step:(i + 1) * step, :], w_f32[k][i * step:(i + 1) * step, :],
                       reads=list(after) if i == 0 else [], writes=[B_scr[k]] if i == 0 else [], sem_buf=B_scr[k], q="pool")
            B_scr[k].last_w = ("d", B_scr[k].dsem, B_scr[k].dcnt)

        SCRATCH = ("wgate", "woa", "wob", "wo", "wpg", "wple", "wup", "wdown")

        def wload(dst, key, src_view, bufw, q="sp"):
            Sd.dma(dst, src_view, reads=[B_scr[key]], writes=[bufw], q=q, sem_buf=bufw)

        def wload_cast(dst, src_view, bufw):
            Sd.dma(dst, src_view, writes=[bufw], q="pool", sem_buf=bufw)

        def rstd_batch(ss_ap, n_feat, reads, writes):
            Sd.op("dve", lambda e: e.tensor_scalar(out=ss_ap, in0=ss_ap, scalar1=1.0 / n_feat, scalar2=EPS,
                                                   op0=ALU.mult, op1=ALU.add), reads=reads, writes=writes)
            Sd.op("act", lambda e: e.activation(out=ss_ap, in_=ss_ap, func=AF.Sqrt), reads=writes, writes=writes)
            Sd.op("dve", lambda e: e.reciprocal(out=ss_ap, in_=ss_ap), reads=writes, writes=writes)

        def norm_transpose(xsrc, B_x, ntile, rstd_ap, B_rstd, gcol, dstT, B_dst, xn, B_xn, psbanks):
            if rstd_ap is not None:
                for t in range(ntile):
                    if t % 2 == 0:
                        Sd.op("act", lambda e, t=t: e.activation(out=xn[:, t, :], in_=xsrc[:, t, :], func=AF.Identity,
                                                                 scale=rstd_ap[:, t:t + 1]),
                              reads=[B_x[t], B_rstd], writes=[B_xn[t]])
                    else:
                        Sd.op("dve", lambda e, t=t: e.tensor_scalar(out=xn[:, t, :], in0=xsrc[:, t, :], scalar1=rstd_ap[:, t:t + 1],
                                                                    scalar2=None, op0=ALU.mult),
                              reads=[B_x[t], B_rstd], writes=[B_xn[t]])
            ngrp = (ntile + 7) // 8
            k = 0
            for g in range(ngrp):
                t0 = g * 8
                tn = min(8, ntile - t0)
                for c in range(8):
                    bk = psbanks[k % len(psbanks)]
                    k += 1

                    def tr(e, c=c, t0=t0, tn=tn, bk=bk):
                        ins = None
                        for i in range(tn):
                            ins = e.transpose(out=ps_bf(bk)[:, i * 128:(i + 1) * 128],
                                              in_=xn[:, t0 + i, c * 128:(c + 1) * 128], identity=ident)
                        return ins
                    Sd.op("pe", tr, reads=[B_xn[t0 + i] for i in range(tn)] + [B_const], writes=[B_ps[bk]])
                    dst_bufs = sorted(set(B_dst[(t0 + i) // 4] for i in range(tn)), key=id)
                    Sd.op("dve", lambda e, c=c, t0=t0, tn=tn, bk=bk: e.tensor_scalar(
                        out=dstT[:, c, t0 * 128:(t0 + tn) * 128], in0=ps_bf(bk)[:, 0:tn * 128],
                        scalar1=gvec[:, gcol + c:gcol + c + 1], scalar2=None, op0=ALU.mult),
                        reads=[B_ps[bk], B_const], writes=dst_bufs)

        XALL_OFF = GEN_OFF + 44 * KB
        pre_x = None
        for b in range(NSEQ):
            if stop is not None and (b > 0 or stop == 0):
                break
            row0 = b * S
            Sd.barrier()
            AR.at(HT_OFF)
            hT = AR.take([128, 8, S], BF16)
            tabs = AR.take([128, 4, S], BF16)
            cosA, sinA, cosB, sinB = tabs[:, 0, :], tabs[:, 1, :], tabs[:, 2, :], tabs[:, 3, :]
            B_hT = Sd.bufs("hT", 4)
            B_tabs = Sd.buf("tabs")
            AR.at(GEN_OFF)
            regX = AR.take([128, 32768 // 2], BF16)
            wprep = regX[:, 0:8 * C_END].rearrange("p (c n) -> p c n", c=8)
            wqb = AR.take([128, 2, 768], BF16)
            wqbsw = AR.take([128, 2, 768], BF16)
            wkvb = AR.take([128, 1024], BF16)
            wkpe = AR.take([128, 8, 256], BF16)
            W2_END = AR.off
            B_w = Sd.buf("wprep")
            assert AR.off == XALL_OFF, AR.off
            xall = AR.take([128, NT, D], F32)
            xn = AR.take([128, NT, D], BF16)
            junk = AR.take([128, D], BF16)
            B_x = pre_x["bufs"] if pre_x else Sd.bufs("x", NT)
            B_xn = Sd.bufs("xn", NT)
            B_junk = Sd.buf("junk")
            for t in range(pre_x["n"] if pre_x else 0, NT):
                Sd.dma(xall[:, t, :], x_d[row0 + t * 128: row0 + (t + 1) * 128, :], writes=[B_x[t]])
            pre_x = None
            for dst_, src_ in ((wprep, kview(w_f32["wprep"])), (wqb, kview(w_f32["wqb"])), (wqbsw, kview(w_f32["wqbsw"])),
                               (wkvb, w_f32["wkvb"]), (wkpe, kview(w_f32["wkpe"]))):
                Sd.dma(dst_, src_, reads=[B_x[NT // 2 - 1]], writes=[B_w], q="pool", sem_buf=B_w)
            Sd.dma(tabs, tabs_d.rearrange("a p n -> p a n"), writes=[B_tabs])
            for t in range(NT):
                Sd.op("act", lambda e, t=t: e.activation(out=junk, in_=xall[:, t, :], func=AF.Square,
                                                         accum_out=stat[:, t:t + 1]),
                      reads=[B_x[t]], writes=[B_junk, B_stat])
            rstd_batch(stat[:, 0:NT], D, [B_stat], [B_stat])
            norm_transpose(xall, B_x, NT, stat, B_stat, G_MIX, hT, B_hT, xn, B_xn, [0, 1, 2, 3])

            if stop == 1:
                break
            Sd.barrier()
            AR.at(GEN_OFF)
            AR.at(W2_END)
            cqT = AR.take([128, 2, S], BF16)
            ckvT = AR.take([128, S], BF16)
            QTA = [AR.take([128, S], BF16) for _ in range(2)]
            KTA = [AR.take([128, S], BF16) for _ in range(2)]
            VA = [AR.take([128, NT, 128], BF16) for _ in range(2)]
            QTB = AR.take([128, 4, S], BF16)
            KTB = AR.take([128, S], BF16)
            VB = AR.take([128, NT, 2, 192], BF16)
            PT = [AR.take([128, 2, 512], BF16) for _ in range(3)]
            rec = [AR.take([128, 512], F32) for _ in range(4)]
            rec2 = [AR.take([128, 512], F32) for _ in range(4)]
            sq = [AR.take([128, 512], BF16) for _ in range(3)]
            vr = [AR.take([128, 512], F32) for _ in range(2)]
            t1 = [AR.take([128, 512], F32) for _ in range(2)]
            t2 = [AR.take([128, 512], F32) for _ in range(2)]
            B_cq = Sd.bufs("cq", 4)
            B_ckv = Sd.bufs("ckv", 4)
            B_QTA = [Sd.bufs("qta", 4), Sd.bufs("qta_", 4)]
            B_KTA = [Sd.bufs("kta", 4), Sd.bufs("kta_", 4)]
            B_kpe = [Sd.bufs("kpe", 4), Sd.bufs("kpe_", 4)]
            B_VA = [Sd.buf("va0"), Sd.buf("va1")]
            B_QTB = Sd.bufs("qtb", 16)
            B_KTB = Sd.bufs("ktb", 4)
            B_VB = Sd.buf("vb")
            B_PT = Sd.bufs("pt", 3)
            B_rec = Sd.bufs("rec", 4)
            B_rec2 = Sd.bufs("rec2", 4)
            B_sq = Sd.bufs("sq", 3)
            B_vr = Sd.bufs("vr", 2)
            B_t1 = Sd.bufs("t1", 2)
            B_t2 = Sd.bufs("t2", 2)

            if b == 0:
                for k in SCRATCH:
                    convert(k, after=[B_w])
            Sd.op("pool", lambda e: e.memset(VA[0][:, :, 64:128], 1.0), writes=[B_VA[0]])
            Sd.op("pool", lambda e: e.memset(VA[1][:, :, 0:64], 1.0), writes=[B_VA[1]])
            Sd.op("pool", lambda e: e.memset(VB[:, :, :, 0:64], 1.0), writes=[B_VB])
            Sd.op("pool", lambda e: e.memset(VB[:, :, :, 128:192], 1.0), writes=[B_VB])

            rr = [0]

            def nxt(n=2):
                rr[0] += 1
                return rr[0] % n

            def proj_fm(bank, wcols_fn, tb, kch=8):
                Sd.op("pe", lambda e: mm(e, ps(bank)[0:wcols_fn(0).shape[1], :],
                                         [(wcols_fn(c), hT[:, c, tb * 512:(tb + 1) * 512]) for c in range(kch)]),
                      reads=[B_hT[tb], B_w], writes=[B_ps[bank]])

            def rstd_big(iv, ssb, nfeat):
                if os.environ.get("MK_NOLN"):
                    Sd.op("dve", lambda e: e.tensor_scalar(out=vr[iv], in0=ps(ssb), scalar1=1.0 / nfeat, scalar2=EPS,
                                                           op0=ALU.mult, op1=ALU.add), reads=[B_ps[ssb]], writes=[B_vr[iv]])
                    Sd.op("act", lambda e: e.activation(out=vr[iv], in_=vr[iv], func=AF.Sqrt), reads=[B_vr[iv]], writes=[B_vr[iv]])
                    Sd.op("dve", lambda e: e.reciprocal(out=vr[iv], in_=vr[iv]), reads=[B_vr[iv]], writes=[B_vr[iv]])
                    return
                Sd.op("dve", lambda e: e.tensor_scalar(out=vr[iv], in0=ps(ssb), scalar1=1.0 / nfeat, scalar2=EPS,
                                                       op0=ALU.mult, op1=ALU.add), reads=[B_ps[ssb]], writes=[B_vr[iv]])
                Sd.op("act", lambda e: e.activation(out=vr[iv], in_=vr[iv], func=AF.Ln), reads=[B_vr[iv]], writes=[B_vr[iv]])
                Sd.op("act", lambda e: e.activation(out=vr[iv], in_=vr[iv], func=AF.Exp, scale=-0.5),
                      reads=[B_vr[iv]], writes=[B_vr[iv]])

            for tb in range(4):
                tsl = slice(tb * 512, (tb + 1) * 512)
                for j in range(3):
                    proj_fm(j, lambda c, j=j: wprep[:, c, j * 128:(j + 1) * 128], tb)
                    Sd.op("act", lambda e, j=j: e.activation(out=sq[j], in_=ps(j), func=AF.Square),
                          reads=[B_ps[j]], writes=[B_sq[j]])
                Sd.op("pe", lambda e: mm(e, ps(3), [(ones_all, sq[0]), (ones_all, sq[1])]),
                      reads=[B_sq[0], B_sq[1], B_const], writes=[B_ps[3]])
                Sd.op("pe", lambda e: mm(e, ps(4), [(ones_all, sq[2])]),
                      reads=[B_sq[2], B_const], writes=[B_ps[4]])
                rstd_big(0, 3, 256)
                rstd_big(1, 4, 128)
                for bk, dst, gc, bd, iv in ((0, cqT[:, 0, tsl], G_QA, B_cq[tb], 0), (1, cqT[:, 1, tsl], G_QA + 1, B_cq[tb], 0),
                                            (2, ckvT[:, tsl], G_KVA, B_ckv[tb], 1)):
                    Sd.op("dve", lambda e, bk=bk, dst=dst, gc=gc, iv=iv: e.scalar_tensor_tensor(
                        out=dst, in0=ps(bk), scalar=gvec[:, gc:gc + 1], in1=vr[iv], op0=ALU.mult, op1=ALU.mult),
                        reads=[B_ps[bk], B_vr[iv], B_const], writes=[bd])

            for tb in range(4):
                tsl = slice(tb * 512, (tb + 1) * 512)
                proj_fm(5, lambda c: wkpe[:, c, 0:128], tb)
                proj_fm(6, lambda c: wkpe[:, c, 128:256], tb)
                i1 = nxt()
                Sd.op("dve", lambda e, i1=i1: e.tensor_tensor(out=t1[i1][64:96, :], in0=ps(5)[64:96, :],
                                                              in1=cosA[64:96, tsl], op=ALU.mult),
                      reads=[B_ps[5], B_tabs], writes=[B_t1[i1]])
                Sd.op("dve", lambda e, i1=i1: e.tensor_tensor(out=t2[i1][64:96, :], in0=ps(6)[64:96, :],
                                                              in1=sinA[64:96, tsl], op=ALU.mult),
                      reads=[B_ps[6], B_tabs], writes=[B_t2[i1]])
                for sl in range(2):
                    Sd.op(PEL, lambda e, i1=i1, sl=sl: e.tensor_tensor(out=KTA[sl][64:96, tsl], in0=t1[i1][64:96, :],
                                                                          in1=t2[i1][64:96, :], op=ALU.add),
                          reads=[B_t1[i1], B_t2[i1]], writes=[B_kpe[sl][tb]])

            def prep_b(colA, colS, gcol, gcolS, dst_fn, bdst_fn):
                for tb in range(4):
                    tsl = slice(tb * 512, (tb + 1) * 512)
                    bA, bS, bSS = 0 + 3 * (tb % 2), 1 + 3 * (tb % 2), 2 + 3 * (tb % 2)
                    proj_fm(bA, lambda c: wprep[:, c, colA:colA + 128], tb)
                    proj_fm(bS, lambda c: wprep[:, c, colS:colS + 128], tb)
                    i = nxt()
                    Sd.op("act", lambda e, i=i, bA=bA: e.activation(out=sq[i], in_=ps(bA), func=AF.Square),
                          reads=[B_ps[bA]], writes=[B_sq[i]])
                    Sd.op("pe", lambda e, i=i, bSS=bSS: mm(e, ps(bSS), [(ones_blk, sq[i])]),
                          reads=[B_sq[i], B_const], writes=[B_ps[bSS]])
                    rstd_big(i, bSS, 64)
                    Sd.op("dve", lambda e, i=i, bA=bA: e.scalar_tensor_tensor(
                        out=t1[i], in0=ps(bA), scalar=gvec[:, gcol:gcol + 1], in1=cosB[:, tsl], op0=ALU.mult, op1=ALU.mult),
                        reads=[B_ps[bA], B_const, B_tabs], writes=[B_t1[i]])
                    Sd.op("dve", lambda e, i=i, bS=bS: e.scalar_tensor_tensor(
                        out=t2[i], in0=ps(bS), scalar=gvec[:, gcolS:gcolS + 1], in1=sinB[:, tsl], op0=ALU.mult, op1=ALU.mult),
                        reads=[B_ps[bS], B_const, B_tabs], writes=[B_t2[i]])
                    Sd.op(PEL, lambda e, i=i: e.tensor_tensor(out=t1[i], in0=t1[i], in1=t2[i], op=ALU.add),
                          reads=[B_t1[i], B_t2[i]], writes=[B_t1[i]])
                    Sd.op(PEL, lambda e, i=i, tb=tb: e.tensor_tensor(out=dst_fn(tsl), in0=t1[i], in1=vr[i], op=ALU.mult),
                          reads=[B_t1[i], B_vr[i]], writes=[bdst_fn(tb)])

            for j in range(4):
                prep_b(C_QB + j * 128, C_QBSW + j * 128, G_QN, G_QNSW,
                       lambda tsl, j=j: QTB[:, j, tsl], lambda tb, j=j: B_QTB[j * 4 + tb])
            prep_b(C_KB, C_KBSW, G_KN, G_KNSW, lambda tsl: KTB[:, tsl], lambda tb: B_KTB[tb])

            for t in range(NT):
                bk = 6 + (t % 2)
                Sd.op("pe", lambda e, t=t, bk=bk: mm(e, ps(bk)[:, 0:128],
                                                     [(hT[:, c, t * 128:(t + 1) * 128], wprep[:, c, C_VB:C_VB + 128])
                                                      for c in range(8)]),
                      reads=[B_hT[t // 4], B_w], writes=[B_ps[bk]])
                Sd.op("dve", lambda e, t=t, bk=bk: e.tensor_copy(
                    out=VB[:, t, :, 64:128], in_=ps(bk)[:, 0:128].rearrange("p (g d) -> p g d", g=2)),
                    reads=[B_ps[bk]], writes=[B_VB])

            if stop == 2:
                break
            Sd.barrier()
            oaT = regX[:, 0:4 * S].rearrange("p (c n) -> p c n", c=4)
            obT = regX[:, 4 * S:8 * S].rearrange("p (c n) -> p c n", c=4)
            B_oa = Sd.bufs("oa", 16)
            B_ob = Sd.bufs("ob", 16)

            def prep_a_pieces(h):
                sl = h % 2
                pieces = []
                for tb in range(4):
                    tsl = slice(tb * 512, (tb + 1) * 512)

                    def pa(banks, tb=tb, tsl=tsl):
                        b0, b1 = banks
                        Sd.op("pe", lambda e: mm(e, ps(b0)[0:96, :], [(wqb[:, c, h * 96:(h + 1) * 96], cqT[:, c, tsl])
                                                                      for c in range(2)]),
                              reads=[B_cq[tb], B_w], writes=[B_ps[b0]])
                        Sd.op("pe", lambda e: mm(e, ps(b1)[0:96, :], [(wqbsw[:, c, h * 96:(h + 1) * 96], cqT[:, c, tsl])
                                                                      for c in range(2)]),
                              reads=[B_cq[tb], B_w], writes=[B_ps[b1]])
                        i1 = nxt()
                        Sd.op("dve", lambda e: e.tensor_copy(out=QTA[sl][0:64, tsl], in_=ps(b0)[0:64, :]),
                              reads=[B_ps[b0]], writes=[B_QTA[sl][tb]])
                        Sd.op("dve", lambda e: e.tensor_tensor(out=t1[i1][64:96, :], in0=ps(b0)[64:96, :],
                                                               in1=cosA[64:96, tsl], op=ALU.mult),
                              reads=[B_ps[b0], B_tabs], writes=[B_t1[i1]])
                        Sd.op("dve", lambda e: e.tensor_tensor(out=t2[i1][64:96, :], in0=ps(b1)[64:96, :],
                                                               in1=sinA[64:96, tsl], op=ALU.mult),
                              reads=[B_ps[b1], B_tabs], writes=[B_t2[i1]])
                        Sd.op(PEL, lambda e: e.tensor_tensor(out=QTA[sl][64:96, tsl], in0=t1[i1][64:96, :],
                                                             in1=t2[i1][64:96, :], op=ALU.add),
                              reads=[B_t1[i1], B_t2[i1]], writes=[B_QTA[sl][tb]])
                    pieces.append(pa)
                vo = 0 if sl == 0 else 64
                for t4 in range(4):
                    tsl = slice(t4 * 512, (t4 + 1) * 512)

                    def pkv(banks, t4=t4, tsl=tsl):
                        b0, b1 = banks
                        Sd.op("pe", lambda e: mm(e, ps(b0)[0:64, :], [(wkvb[:, h * 128:h * 128 + 64], ckvT[:, tsl])]),
                              reads=[B_ckv[t4], B_w], writes=[B_ps[b0]])
                        Sd.op("dve", lambda e: e.tensor_copy(out=KTA[sl][0:64, tsl], in_=ps(b0)[0:64, :]),
                              reads=[B_ps[b0]], writes=[B_KTA[sl][t4]])

                        def vmm(e):
                            ins = None
                            for i in range(4):
                                t = t4 * 4 + i
                                ins = e.matmul(ps(b1)[:, i * 64:(i + 1) * 64], ckvT[:, t * 128:(t + 1) * 128],
                                               wkvb[:, h * 128 + 64:h * 128 + 128], start=True, stop=True)
                            return ins
                        Sd.op("pe", vmm, reads=[B_ckv[t4], B_w], writes=[B_ps[b1]])
                        Sd.op("dve", lambda e: e.tensor_copy(
                            out=VA[sl][:, t4 * 4:(t4 + 1) * 4, vo:vo + 64],
                            in_=ps(b1)[:, 0:256].rearrange("p (t d) -> p t d", t=4)),
                            reads=[B_ps[b1]], writes=[B_VA[sl]])
                    pieces.append(pkv)
                return pieces

            nrm = [0]

            def attention(steps, scale, pairs, look, fillers=None, ndummy=0):
                n = len(steps)
                fillers = fillers or {}
                free = [[p_, -1] for p_ in pairs]
                nq = [0]
                pending = []

                def qk(i, banks):
                    s_ = steps[i]
                    s_["banks"] = banks

                    def f(e):
                        ins = None
                        for jj in range(2):
                            ins = e.matmul(ps(banks[jj]), s_["qk"][jj][0], s_["qk"][jj][1], start=True, stop=True)
                        return ins
                    Sd.op("pe", f, reads=s_["rQK"], writes=[B_ps[banks[0]], B_ps[banks[1]]])
                    if ndummy:
                        def dm(e):
                            ins = None
                            for jj in range(ndummy):
                                ins = e.matmul(ps(banks[jj % 2]), zeros_w, cosB[:, 0:512], start=False, stop=(jj == ndummy - 1))
                            return ins
                        Sd.op("pe", dm, reads=[B_const, B_tabs], writes=[B_ps[banks[0]], B_ps[banks[1]]])

                def issue_qk(i, force):
                    while nq[0] < n and nq[0] <= i + look:
                        must = force and nq[0] <= i
                        cand = [f_ for f_ in free if f_[1] <= i or must]
                        if not cand:
                            assert not must
                            break
                        cand.sort(key=lambda f_: f_[1])
                        free.remove(cand[0])
                        qk(nq[0], cand[0][0])
                        nq[0] += 1

                for i in range(n):
                    s_ = steps[i]
                    issue_qk(i, True)
                    banks = s_["banks"]
                    assert banks[1] == banks[0] + 1
                    pi = i % 3
                    Sd.op("act", lambda e, banks=banks, pi=pi: e.activation(
                        out=PT[pi].rearrange("p a b -> p (a b)"), in_=ps2(banks[0]), func=AF.Exp, scale=scale),
                        reads=[B_ps[banks[0]], B_ps[banks[1]]], writes=[B_PT[pi]])

                    def pv(e, s_=s_, pi=pi):
                        ins = None
                        for jj in range(2):
                            ob, v_ap, st_, sp_ = s_["pv"][jj]
                            ins = e.matmul(ps(ob), v_ap, PT[pi][:, jj, :], start=st_, stop=sp_)
                        return ins
                    Sd.op("pe", pv, reads=[B_PT[pi]] + s_["rV"], writes=sorted(set(B_ps[o_[0]] for o_ in s_["pv"]), key=id))
                    for ob, par, dst, bdst in s_.get("fin", ()):
                        orow = slice(par * 64, par * 64 + 64)
                        drow = slice((1 - par) * 64, (1 - par) * 64 + 64)
                        ri = nrm[0] % 4
                        nrm[0] += 1
                        Sd.op("dve", lambda e, ob=ob, ri=ri, drow=drow: e.reciprocal(out=rec[ri][drow, :], in_=ps(ob)[drow, :]),
                              reads=[B_ps[ob]], writes=[B_rec[ri]])
                        Sd.dma(rec2[ri][orow, :], rec[ri][drow, :], reads=[B_rec[ri]], writes=[B_rec2[ri]])

                        def fin_mult(ob=ob, ri=ri, orow=orow, dst=dst, bdst=bdst):
                            Sd.op("dve", lambda e: e.tensor_tensor(
                                out=dst[orow, :], in0=ps(ob)[orow, :], in1=rec2[ri][orow, :], op=ALU.mult),
                                reads=[B_ps[ob], B_rec2[ri]], writes=[bdst])
                        pending.append((i + 5, fin_mult))
                    free.append([banks, i])
                    for f_ in fillers.get(i, ()):
                        cand = sorted(free, key=lambda f__: f__[1])
                        if cand:
                            free.remove(cand[0])
                            f_(cand[0][0])
                            free.append([cand[0][0], i + 3])
                        else:
                            fillers.setdefault(i + 1, []).append(f_)
                    for it_ in [p_ for p_ in pending if p_[0] <= i]:
                        pending.remove(it_)
                        it_[1]()
                for it_ in pending:
                    it_[1]()

            pairsA = [(0, 1), (2, 3), (4, 5)]
            for k_, pc in enumerate(prep_a_pieces(0)):
                pc(pairsA[k_ % 3])
            steps = []
            fillers = {}
            unit = 0
            for h in range(8):
                sl = h % 2
                base = len(steps)
                if h + 1 < 8:
                    pcs = prep_a_pieces(h + 1)
                    order = [pcs[0], pcs[4], pcs[5], pcs[6], pcs[7], pcs[1], pcs[2], pcs[3]]
                    for off_, pc in zip((2, 5, 10, 13, 18, 21, 26, 29), order):
                        fillers.setdefault(base + off_, []).append(pc)
                for qb_ in range(4):
                    ob = 6 + (unit % 2)
                    for kp in range(8):
                        st_ = dict(
                            qk=[(KTA[sl][0:96, (kp * 2 + jj) * 128:(kp * 2 + jj + 1) * 128],
                                 QTA[sl][0:96, qb_ * 512:(qb_ + 1) * 512]) for jj in range(2)],
                            rQK=[B_QTA[sl][qb_], B_KTA[sl][kp // 2], B_kpe[sl][kp // 2]],
                            pv=[(ob, VA[sl][:, kp * 2 + jj, :], kp == 0 and jj == 0, kp == 7 and jj == 1) for jj in range(2)],
                            rV=[B_VA[sl]])
                        if kp == 7:
                            st_["fin"] = [(ob, h % 2, oaT[:, h // 2, qb_ * 512:(qb_ + 1) * 512], B_oa[(h // 2) * 4 + qb_])]
                        steps.append(st_)
                    unit += 1
            attention(steps, SCALE_A, pairsA, LOOKA, fillers, ndummy=NDUMMY_A)
            AR.at(GEN_OFF + 32 * KB, MERGED_OFF)
            wg = [AR.take([128, 8, 2, 128], BF16) for _ in range(2)]
            woab = [AR.take([128, 4, 2, 128], BF16) for _ in range(2)]
            B_wg = Sd.bufs("wg", 2)
            B_woab = Sd.bufs("woab", 2)
            wgate_v = w_bf["wgate"].rearrange("(c p) (g n) -> p c g n", p=128, g=2)

            def load_merge_w(c, extra=()):
                wi = c % 2
                for g_ in range(2):
                    Sd.dma(wg[wi][:, :, g_, :], wgate_v[:, :, g_, c * 128:(c + 1) * 128], reads=[B_scr["wgate"]],
                           writes=[B_wg[wi]] + (list(extra) if g_ == 0 else []), sem_buf=B_wg[wi])
                Sd.dma(woab[wi][:, :, 0, :], w_bf["woa"].rearrange("(j p) n -> p j n", p=128)[:, :, c * 128:(c + 1) * 128],
                       reads=[B_scr["woa"]], writes=[B_woab[wi]], sem_buf=B_woab[wi])
                Sd.dma(woab[wi][:, :, 1, :], w_bf["wob"].rearrange("(j p) n -> p j n", p=128)[:, :, c * 128:(c + 1) * 128],
                       reads=[B_scr["wob"]], writes=[B_woab[wi]], sem_buf=B_woab[wi])

            if not BOLD:
                load_merge_w(0, extra=[B_w])
                load_merge_w(1)
            if BOLD:
                steps = []
                unit = 0
                for h in range(8):
                    g, j = h // 4, h % 4
                    voff = 64 if h % 2 == 0 else 0
                    for qb_ in range(4):
                        ob = 6 + (unit % 2)
                        qsl = slice(qb_ * 512, (qb_ + 1) * 512)
                        for kp in range(8):
                            st_ = dict(
                                qk=[(KTB[g * 64:(g + 1) * 64, (kp * 2 + jj) * 128:(kp * 2 + jj + 1) * 128],
                                     QTB[g * 64:(g + 1) * 64, j, qsl]) for jj in range(2)],
                                rQK=[B_QTB[j * 4 + qb_], B_KTB[kp // 2]],
                                pv=[(ob, VB[:, kp * 2 + jj, g, voff:voff + 128], kp == 0 and jj == 0, kp == 7 and jj == 1) for jj in range(2)],
                                rV=[B_VB])
                            if kp == 7:
                                st_["fin"] = [(ob, h % 2, obT[:, h // 2, qsl], B_ob[(h // 2) * 4 + qb_])]
                            steps.append(st_)
                        unit += 1
                attention(steps, SCALE_B, pairsA, 2)
            else:
                steps = []
                unit = 0
                for j in range(4):
                    for qb_ in range(4):
                        obs = (4 + 2 * (unit % 2), 5 + 2 * (unit % 2))
                        qsl = slice(qb_ * 512, (qb_ + 1) * 512)
                        voff = 64 if j % 2 == 0 else 0
                        for kt in range(NT):
                            ksl = slice(kt * 128, (kt + 1) * 128)
                            st_ = dict(
                                qk=[(KTB[g * 64:(g + 1) * 64, ksl], QTB[g * 64:(g + 1) * 64, j, qsl]) for g in range(2)],
                                rQK=[B_QTB[j * 4 + qb_], B_KTB[kt // 4]],
                                pv=[(obs[g], VB[:, kt, g, voff:voff + 128], kt == 0, kt == NT - 1) for g in range(2)],
                                rV=[B_VB])
                            if kt == NT - 1:
                                st_["fin"] = [(obs[g], (j + 4 * g) % 2, obT[:, (j + 4 * g) // 2, qsl], B_ob[((j + 4 * g) // 2) * 4 + qb_])
                                              for g in range(2)]
                            steps.append(st_)
                        unit += 1
                attention(steps, SCALE_B, [(0, 1), (2, 3)], 1)

            if stop == 3:
                break
            Sd.barrier()
            WDN_OFF = 108 * KB
            AR.at(WDN_OFF)
            wdn = [AR.take([128, 8, D], BF16) for _ in range(2)]
            B_wdn = Sd.bufs("wdn", 2)
            AR.at(MERGED_OFF)
            mergedT = AR.take([128, 8, S], BF16)
            AR.at(GEN_OFF + 32 * KB + 12 * KB, MERGED_OFF)
            tg = [AR.take([128, 512], F32) for _ in range(4)]
            mm_ = [AR.take([128, 512], F32) for _ in range(4)]
            B_tg = Sd.bufs("tg", 4)
            B_mm = Sd.bufs("mm", 4)
            B_mg = Sd.bufs("mg", 4)
            for c in range(8):
                wi = c % 2
                if c >= 2 or BOLD:
                    load_merge_w(c)
                if c == 1:
                    wload(wdn[1], "wo", kview(w_bf["wo"]), B_wdn[1])
                    wload(wdn[0], "wdown", kview(w_bf["wdown"][0:1024, :]), B_wdn[0])
                for tb in range(4):
                    tsl = slice(tb * 512, (tb + 1) * 512)
                    pb = 4 * (tb % 2)
                    for g_ in range(2):
                        Sd.op("pe", lambda e, g_=g_, pb=pb: mm(e, ps(pb + g_), [(wg[wi][:, k, g_, :], hT[:, k, tsl]) for k in range(8)]),
                              reads=[B_hT[tb], B_wg[wi]], writes=[B_ps[pb + g_]])
                    Sd.op("pe", lambda e, pb=pb: mm(e, ps(pb + 2), [(woab[wi][:, j, 0, :], oaT[:, j, tsl]) for j in range(4)]),
                          reads=[B_oa[j * 4 + tb] for j in range(4)] + [B_woab[wi]], writes=[B_ps[pb + 2]])
                    Sd.op("pe", lambda e, pb=pb: mm(e, ps(pb + 3), [(woab[wi][:, j, 1, :], obT[:, j, tsl]) for j in range(4)]),
                          reads=[B_ob[j * 4 + tb] for j in range(4)] + [B_woab[wi]], writes=[B_ps[pb + 3]])
                    ia, ib = 2 * (tb % 2), 2 * (tb % 2) + 1
                    for g_, ii in ((0, ia), (1, ib)):
                        Sd.op("act", lambda e, g_=g_, ii=ii, pb=pb: e.activation(out=tg[ii], in_=ps(pb + g_), func=AF.Tanh, scale=0.5),
                              reads=[B_ps[pb + g_]], writes=[B_tg[ii]])
                        Sd.op("dve", lambda e, g_=g_, ii=ii, pb=pb: e.scalar_tensor_tensor(
                            out=mm_[ii], in0=tg[ii], scalar=1.0, in1=ps(pb + 2 + g_), op0=ALU.add, op1=ALU.mult),
                            reads=[B_tg[ii], B_ps[pb + 2 + g_]], writes=[B_mm[ii]])
                    Sd.op(PEL, lambda e, ia=ia, ib=ib: e.tensor_tensor(out=mergedT[:, c, tsl], in0=mm_[ia], in1=mm_[ib], op=ALU.add),
                          reads=[B_mm[ia], B_mm[ib]], writes=[B_mg[tb]])

            if stop == 4:
                break
            Sd.barrier()
            NB = 8
            AR.at(0, MERGED_OFF)
            x1 = AR.take([128, NB, D], F32)
            xt = [AR.take([128, D], F32) for _ in range(2)]
            h2T = AR.take([128, 8, NB * 128], BF16)
            uT = AR.take([128, 8, NB * 128], BF16)
            xn5 = uT.rearrange("p a b -> p (a b)").rearrange("p (t n) -> p t n", t=NB)
            wA = AR.take([128, 10, D], BF16)
            wup = [AR.take([128, 8, 512], BF16) for _ in range(2)]
            assert AR.off == WDN_OFF, AR.off
            AR.off += 2 * 16 * KB
            rl = [AR.take([128, 512], F32) for _ in range(2)]
            pbf = AR.take([128, NB, 256], BF16)
            pT = AR.take([128, 2, NB * 128], BF16)
            tgp = AR.take([128, D], F32)
            mp = AR.take([128, D], F32)
            junk5 = AR.take([128, D], BF16)
            ost2 = AR.take([128, D], F32)
            B_x1 = Sd.bufs("x1", NB)
            B_xt = Sd.bufs("xt", 2)
            B_h2 = Sd.bufs("h2T", NB // 4)
            B_uTh = Sd.bufs("uTh", 2)
            B_xn5 = [B_uTh[0], B_uTh[1]] * (NB // 2)
            B_wA = Sd.buf("wA")
            B_wup = Sd.bufs("wup", 2)
            B_rl = Sd.bufs("rl", 2)
            B_pbf = Sd.buf("pbf")
            B_pT = Sd.buf("pT")
            B_tgp = Sd.buf("tgp")
            B_mp = Sd.buf("mp")
            ost = [tgp, mp, ost2]
            B_ost = [B_tgp, B_mp, Sd.buf("ost2")]
            B_junk5 = Sd.buf("junk5")
            B_stat2 = Sd.buf("stat2")
            B_stat3 = Sd.buf("stat3")
            wo_v = wdn[1]
            wpg_v = wA[:, 0:8, :]
            wple_v = wA[:, 8:10, :]
            wup_v = kview(w_bf["wup"])
            NBLK = NT // NB

            def cast_tile(t):
                if t % 2 == 0:
                    Sd.op("act", lambda e: e.activation(out=xn5[:, t, :], in_=x1[:, t, :], func=AF.Identity),
                          reads=[B_x1[t]], writes=[B_uTh[0], B_uTh[1]])
                else:
                    Sd.op("dve", lambda e: e.tensor_copy(out=xn5[:, t, :], in_=x1[:, t, :]),
                          reads=[B_x1[t]], writes=[B_uTh[0], B_uTh[1]])

            def statsN(col0):
                for t in range(NB):
                    Sd.op("act", lambda e, t=t: e.activation(out=junk5, in_=x1[:, t, :], func=AF.Square,
                                                             accum_out=stat[:, col0 + t:col0 + t + 1]),
                          reads=[B_x1[t]], writes=[B_junk5, B_stat])
                rstd_batch(stat[:, col0:col0 + NB], D, [B_stat], [B_stat])

            def load_up(blk_, q_, g_):
                wi = g_ % 2
                f0 = q_ * 1024 + g_ * 512
                wload(wup[wi], "wup", wup_v[:, :, f0:f0 + 512], B_wup[wi])

            def load_dn(q_):
                wload(wdn[q_ % 2], "wdown", kview(w_bf["wdown"][q_ * 1024:(q_ + 1) * 1024, :]), B_wdn[q_ % 2])

            wload(wpg_v, "wpg", kview(w_bf["wpg"]), B_wA)
            wload(wple_v, "wple", kview(w_bf["wple"]), B_wA)
            for blk in range(NBLK):
                r0 = row0 + blk * NB * 128
                tok0 = blk * NB * 128
                load_up(blk, 0, 0)
                load_up(blk, 0, 1)
                if blk > 0:
                    load_dn(0)
                for t in range(NB):
                    xi = t % 2
                    Sd.dma(xt[xi], x_d[r0 + t * 128:r0 + (t + 1) * 128, :], writes=[B_xt[xi]])
                    pb = 2 * (t % 4)

                    def omm(e, t=t, pb=pb):
                        ins = None
                        for cb in range(2):
                            ins = mm(e, ps(pb + cb), [(mergedT[:, k, tok0 + t * 128: tok0 + (t + 1) * 128],
                                                       wo_v[:, k, cb * 512:(cb + 1) * 512]) for k in range(8)])
                        return ins
                    Sd.op("pe", omm, reads=[B_mg[(tok0 + t * 128) // 512], B_wdn[1]], writes=[B_ps[pb], B_ps[pb + 1]])
                    Sd.op("dve", lambda e, t=t, pb=pb, xi=xi: e.scalar_tensor_tensor(
                        out=x1[:, t, :], in0=ps2(pb), scalar=0.5, in1=xt[xi], op0=ALU.mult, op1=ALU.add),
                        reads=[B_ps[pb], B_ps[pb + 1], B_xt[xi]], writes=[B_x1[t]])
                    cast_tile(t)
                Sd.dma(pbf, p_d[r0:r0 + NB * 128, :].rearrange("(t p) n -> p t n", p=128), writes=[B_pbf], q="pool")
                norm_transpose(x1, B_x1, NB, None, B_stat, G_MLP, h2T, B_h2, xn5, B_xn5, [4, 5, 6, 7])
                statsN(16)
                Sd.op("dve", lambda e: e.tensor_tensor(out=stat[:, 40:40 + NB], in0=stat[:, 16:16 + NB], in1=stat[:, 16:16 + NB], op=ALU.mult),
                      reads=[B_stat], writes=[B_stat2])
                for q_ in range(4):
                    cnt = 0
                    for th in range(NB // 4):
                        for g_ in range(2):
                            wi = g_ % 2
                            for fi in range(4):
                                fc = g_ * 4 + fi
                                bk = cnt % 4
                                ri = cnt % 2
                                cnt += 1
                                Sd.op("pe", lambda e, fi=fi, bk=bk, th=th: mm(e, ps(bk), [
                                    (wup[wi][:, k, fi * 128:(fi + 1) * 128], h2T[:, k, th * 512:(th + 1) * 512]) for k in range(8)]),
                                    reads=[B_h2[th], B_wup[wi]], writes=[B_ps[bk]])
                                Sd.op("act", lambda e, bk=bk, ri=ri: e.activation(out=rl[ri], in_=ps(bk), func=AF.Relu),
                                      reads=[B_ps[bk]], writes=[B_rl[ri]])
                                Sd.op("dve", lambda e, fc=fc, ri=ri, th=th: e.tensor_tensor(
                                    out=uT[:, fc, th * 512:(th + 1) * 512], in0=rl[ri], in1=rl[ri], op=ALU.mult),
                                    reads=[B_rl[ri]], writes=[B_uTh[th]])
                            if th == NB // 4 - 1 and q_ + 1 < 4:
                                load_up(blk, q_ + 1, g_)
                    for t in range(NB):
                        pb = 4 + 2 * (t % 2)

                        def dmm(e, t=t, pb=pb):
                            ins = None
                            for cb in range(2):
                                ins = mm(e, ps(pb + cb), [(uT[:, fc, t * 128:(t + 1) * 128], wdn[q_ % 2][:, fc, cb * 512:(cb + 1) * 512])
                                                          for fc in range(8)])
                            return ins
                        Sd.op("pe", dmm, reads=[B_uTh[t // 4], B_wdn[q_ % 2]], writes=[B_ps[pb], B_ps[pb + 1]])
                        Sd.op("dve", lambda e, t=t, pb=pb: e.scalar_tensor_tensor(
                            out=x1[:, t, :], in0=ps2(pb), scalar=stat[:, 40 + t:41 + t], in1=x1[:, t, :], op0=ALU.mult, op1=ALU.add),
                            reads=[B_ps[pb], B_ps[pb + 1], B_x1[t], B_stat2], writes=[B_x1[t]])

                        if t == 0 and q_ + 1 < 4:
                            load_dn(q_ + 1)
                if blk + 1 < NBLK:
                    wload(wo_v, "wo", kview(w_bf["wo"]), B_wdn[1])
                elif b + 1 < NSEQ:
                    save_off, save_lim = AR.off, AR.lim
                    AR.at(XALL_OFF)
                    xall_nx = AR.take([128, NT, D], F32)
                    AR.at(save_off, save_lim)
                    nb_ = Sd.bufs("x", NT)
                    olds = [B_wup[0]] * 2 + [B_wup[1]] * 2 + [B_wdn[0]] * 4 + [B_wdn[1]] * 4
                    for t in range(12):
                        Sd.dma(xall_nx[:, t, :], x_d[(b + 1) * S + t * 128:(b + 1) * S + (t + 1) * 128, :],
                               writes=[nb_[t], olds[t]], sem_buf=nb_[t])
                    pre_x = dict(bufs=nb_, n=12)
                for t in range(NB):
                    cast_tile(t)
                norm_transpose(x1, B_x1, NB, None, B_stat, G_PLE, h2T, B_h2, xn5, B_xn5, [4, 5, 6, 7])
                statsN(24)
                Sd.op("dve", lambda e: e.tensor_scalar(out=stat[:, 48:48 + NB], in0=stat[:, 24:24 + NB], scalar1=0.5, scalar2=None, op0=ALU.mult),
                      reads=[B_stat], writes=[B_stat3])
                for j in range(2):
                    def ptr(e, j=j):
                        ins = None
                        for t in range(NB):
                            ins = e.transpose(out=ps_bf(j)[:, t * 128:(t + 1) * 128],
                                              in_=pbf[:, t, j * 128:(j + 1) * 128], identity=ident)
                        return ins
                    Sd.op("pe", ptr, reads=[B_pbf, B_const], writes=[B_ps[j]])
                    Sd.op("dve", lambda e, j=j: e.tensor_copy(out=pT[:, j, :], in_=ps_bf(j)), reads=[B_ps[j]], writes=[B_pT])
                for t in range(NB):
                    pb = 4 * (t % 2)

                    def gmm(e, t=t, pb=pb):
                        ins = None
                        for cb in range(2):
                            ins = mm(e, ps(pb + cb), [(h2T[:, k, t * 128:(t + 1) * 128], wpg_v[:, k, cb * 512:(cb + 1) * 512])
                                                      for k in range(8)])
                        for cb in range(2):
                            ins = mm(e, ps(pb + 2 + cb), [(pT[:, j, t * 128:(t + 1) * 128], wple_v[:, j, cb * 512:(cb + 1) * 512])
                                                          for j in range(2)])
                        return ins
                    Sd.op("pe", gmm, reads=[B_h2[t // 4], B_pT, B_wA], writes=[B_ps[pb + i] for i in range(4)])
                    Sd.op("act", lambda e, pb=pb, t=t: e.activation(out=tgp, in_=ps2(pb), func=AF.Tanh, scale=stat[:, 48 + t:49 + t]),
                          reads=[B_ps[pb], B_ps[pb + 1], B_stat3], writes=[B_tgp])
                    Sd.op("dve", lambda e, pb=pb: e.scalar_tensor_tensor(
                        out=mp, in0=tgp, scalar=1.0, in1=ps2(pb + 2), op0=ALU.add, op1=ALU.mult),
                        reads=[B_tgp, B_ps[pb + 2], B_ps[pb + 3]], writes=[B_mp])
                    Sd.op("dve", lambda e, t=t: e.scalar_tensor_tensor(
                        out=x1[:, t, :], in0=mp, scalar=0.5, in1=x1[:, t, :], op0=ALU.mult, op1=ALU.add),
                        reads=[B_mp, B_x1[t]], writes=[B_x1[t]])
                statsN(32)
                for t in range(NB):
                    oi = t % 3
                    Sd.op("dve", lambda e, t=t, oi=oi: e.scalar_tensor_tensor(
                        out=ost[oi], in0=x1[:, t, :], scalar=stat[:, 32 + t:33 + t], in1=gfin[:], op0=ALU.mult, op1=ALU.mult),
                        reads=[B_x1[t], B_stat, B_const], writes=[B_ost[oi]])
                    Sd.dma(out_d[r0 + t * 128:r0 + (t + 1) * 128, :], ost[oi], reads=[B_ost[oi]], q="pool")
        Sd.barrier()
    return nc


def _swap_idx(n, half):
    idx = np.arange(n)
    blk = idx // (2 * half)
    r = idx % (2 * half)
    return blk * 2 * half + np.where(r < half, r + half, r - half)


def _tables():
    inv = 10000.0 ** (-np.arange(0, 32, 2, dtype=np.float32) / 32.0)
    t = np.arange(S)
    cosA = np.zeros((128, S), np.float32)
    sinA = np.zeros((128, S), np.float32)
    for i in range(32):
        ang = t.astype(np.float32) * inv[i % 16]
        cosA[64 + i] = np.cos(ang)
        sinA[64 + i] = np.sin(ang) * (-1.0 if i < 16 else 1.0)
    cosB = np.zeros((128, S), np.float32)
    sinB = np.zeros((128, S), np.float32)
    row = (t // 64).astype(np.float32)
    col = (t % 64).astype(np.float32)
    for p in range(128):
        d = p % 64
        pos = row if d < 32 else col
        ang = pos * inv[d % 16]
        cosB[p] = np.cos(ang)
        sinB[p] = np.sin(ang) * (-1.0 if (d % 32) < 16 else 1.0)
    return np.stack([cosA, sinA, cosB, sinB]).astype(ml_dtypes.bfloat16)


def _consts():
    ident = np.eye(128, dtype=np.float32)
    blk = np.zeros((128, 128), np.float32)
    blk[0:64, 0:64] = 1.0
    blk[64:128, 64:128] = 1.0
    allo = np.ones((128, 128), np.float32)
    return np.stack([ident, blk, allo, np.zeros((128, 128), np.float32)]).astype(ml_dtypes.bfloat16)


def _prep_weights(inp):
    w_in = np.asarray(inp["w_in"])[0]
    q_lat = w_in[:, 0:256]
    c_kv = w_in[:, 256:384]
    k_pe = w_in[:, 384:416]
    qb = w_in[:, 416:928]
    kb = w_in[:, 928:1056]
    vb = w_in[:, 1056:1184]
    order = np.array([0, 4, 1, 5, 2, 6, 3, 7])
    qbP = qb.reshape(D, 8, 64)[:, order, :]
    sw64 = _swap_idx(64, 16)
    qbP_sw = qbP[:, :, sw64].reshape(D, 512)
    qbP = qbP.reshape(D, 512)
    kb_sw = kb.reshape(D, 2, 64)[:, :, sw64].reshape(D, 128)
    wprep = np.concatenate([q_lat, c_kv, qbP, kb, vb, qbP_sw, kb_sw], axis=1)
    assert wprep.shape[1] == C_END
    sw32 = _swap_idx(32, 16)
    wkpe = np.zeros((D, 256), np.float32)
    wkpe[:, 64:96] = k_pe
    wkpe[:, 128 + 64:128 + 96] = k_pe[:, sw32]
    wgate = w_in[:, 1184:3232]
    w_qb = np.asarray(inp["w_qb"])[0]
    wq3 = w_qb.reshape(256, 8, 96).copy()
    wq3[:, :, 64:96] = wq3[:, :, 64:96][:, :, sw32]
    w_qb_sw = wq3.reshape(256, 768)

    def gcols(v):
        return np.asarray(v, np.float32).reshape(-1, 128).T

    g_qn = np.asarray(inp["g_qn"])[0]
    g_kn = np.asarray(inp["g_kn"])[0]
    gvec = np.zeros((128, 32), np.float32)
    gvec[:, G_MIX:G_MIX + 8] = gcols(inp["g_mix"][0])
    gvec[:, G_MLP:G_MLP + 8] = gcols(inp["g_mlp"][0])
    gvec[:, G_PLE:G_PLE + 8] = gcols(inp["g_ple"][0])
    gvec[:, G_QA:G_QA + 2] = gcols(inp["g_qa"][0])
    gvec[:, G_KVA] = np.asarray(inp["g_kva"])[0]
    gvec[:, G_QN] = np.tile(g_qn, 2)
    gvec[:, G_QNSW] = np.tile(g_qn[sw64], 2)
    gvec[:, G_KN] = np.tile(g_kn, 2)
    gvec[:, G_KNSW] = np.tile(g_kn[sw64], 2)
    gfin = np.ascontiguousarray(np.broadcast_to(np.asarray(inp["g_final"], np.float32)[None, :], (128, D)))
    c = np.ascontiguousarray
    return dict(
        wprep=c(wprep), wkpe=wkpe, wgate=c(wgate), wqb=c(w_qb), wqbsw=c(w_qb_sw),
        wkvb=c(np.asarray(inp["w_kvb"])[0]), woa=c(np.asarray(inp["w_oa"])[0]), wob=c(np.asarray(inp["w_ob"])[0]),
        wo=c(np.asarray(inp["w_o"])[0]), wup=c(np.asarray(inp["w_up"])[0]), wdown=c(np.asarray(inp["w_down"])[0]),
        wpg=c(np.asarray(inp["w_ple_gate"])[0]), wple=c(np.asarray(inp["w_ple"])[0]),
        gvec=gvec, gfin=gfin, tabs=_tables(), cst=_consts())


def kernel(**inputs):
    x = np.asarray(inputs["x"], np.float32)
    p = np.asarray(inputs["p"], np.float32)[0]
    shared = _prep_weights(inputs)
    nc = build_program()
    in_maps = []
    for c in range(N_CORES):
        m = dict(shared)
        m["x"] = np.ascontiguousarray(x[2 * c:2 * c + 2].reshape(NSEQ * S, D))
        m["p"] = np.ascontiguousarray(p[2 * c:2 * c + 2].reshape(NSEQ * S, 256))
        in_maps.append(m)
    res = run_bass_kernel_spmd(nc, in_maps, core_ids=list(range(N_CORES)))
    out = np.concatenate([np.asarray(r["out"], np.float32).reshape(NSEQ, S, D) for r in res.results], axis=0)
    return out
```
